# Optimizing a Trainium2 kernel written in Bass

```python
import math
import jax, jax.numpy as jnp
from jax import lax
import numpy as np

D_MODEL = 1024
BATCH = 8
SEQ = 4096
DEPTH = 1

MLA_HEADS = 16
MLA_NOPE = 64
MLA_ROPE = 32
MLA_V = 64
Q_LORA = 256
KV_LORA = 128
SWA_Q_HEADS = 16
SWA_KV_HEADS = 2
SWA_GROUP = SWA_Q_HEADS // SWA_KV_HEADS
SWA_HEAD_DIM = 64
WINDOW = 128
Q_BLOCK = 128
ROPE_THETA = 10000.0
D_FF = 4 * D_MODEL
PLE_DIM = 256
N_BRANCHES = 2
NORM_EPS = 1e-6
NEG = -1e30

IN_SPLITS = (Q_LORA, KV_LORA, MLA_ROPE,
             SWA_Q_HEADS * SWA_HEAD_DIM, SWA_KV_HEADS * SWA_HEAD_DIM, SWA_KV_HEADS * SWA_HEAD_DIM,
             N_BRANCHES * D_MODEL)
IN_COLS = sum(IN_SPLITS)
IN_OFFSETS = tuple(int(v) for v in np.cumsum(IN_SPLITS)[:-1])

kernel_name = "hybrid_mla_swa_gated_sandwich"


def rms_norm(x, g):
    xf = x.astype(jnp.float32)
    y = xf * lax.rsqrt(jnp.mean(xf * xf, axis=-1, keepdims=True) + NORM_EPS)
    return (y * g.astype(jnp.float32)).astype(x.dtype)


def rope(x, pos):
    d = x.shape[-1]
    half = d // 2
    inv = jnp.exp(-math.log(ROPE_THETA) * jnp.arange(half, dtype=jnp.float32) * (2.0 / d))
    ang = pos.astype(jnp.float32)[:, None] * inv[None, :]
    cos = jnp.cos(ang)[None, :, None, :]
    sin = jnp.sin(ang)[None, :, None, :]
    xf = x.astype(jnp.float32)
    x1, x2 = xf[..., :half], xf[..., half:]
    return jnp.concatenate([x1 * cos - x2 * sin, x2 * cos + x1 * sin], axis=-1).astype(x.dtype)


def mla_attention(q, k, v):
    B, S, H, D = q.shape
    nb = S // Q_BLOCK
    scale = D ** -0.5
    qb = q.reshape(B, nb, Q_BLOCK, H, D).transpose(1, 0, 2, 3, 4)
    kpos = jnp.arange(S)

    def one_block(args):
        qblk, i = args
        s = jnp.einsum('bqhd,bkhd->bhqk', qblk, k, preferred_element_type=jnp.float32) * scale
        qpos = i * Q_BLOCK + jnp.arange(Q_BLOCK)
        mask = kpos[None, :] <= qpos[:, None]
        s = jnp.where(mask[None, None], s, NEG)
        pr = jax.nn.softmax(s, axis=-1)
        return jnp.einsum('bhqk,bkhd->bqhd', pr.astype(v.dtype), v)

    out = lax.map(one_block, (qb, jnp.arange(nb)))
    return out.transpose(1, 0, 2, 3, 4).reshape(B, S, H, v.shape[-1])


def swa_attention(q, k, v, sinks):
    B, S, HQ, hd = q.shape
    blk = WINDOW
    nb = S // blk
    scale = hd ** -0.5
    qb = q.reshape(B, nb, blk, SWA_KV_HEADS, SWA_GROUP, hd)

    def band(t):
        tb = t.reshape(B, nb, blk, SWA_KV_HEADS, hd)
        prev = jnp.pad(tb[:, :-1], ((0, 0), (1, 0), (0, 0), (0, 0), (0, 0)))
        return jnp.concatenate([prev, tb], axis=2)

    kb, vb = band(k), band(v)
    s = jnp.einsum('bnqhgd,bnkhd->bnhgqk', qb, kb, preferred_element_type=jnp.float32) * scale
    qi = jnp.arange(blk)[:, None]
    kj = jnp.arange(2 * blk)[None, :] - blk
    rel = qi - kj
    band_mask = (rel >= 0) & (rel < WINDOW)
    valid = (jnp.arange(nb)[:, None, None] * blk + kj[None]) >= 0
    mask = band_mask[None] & valid
    s = jnp.where(mask[None, :, None, None], s, NEG)
    sink = sinks.astype(jnp.float32).reshape(SWA_KV_HEADS, SWA_GROUP)[None, None, :, :, None, None]
    m = jnp.maximum(jnp.max(s, axis=-1, keepdims=True), sink)
    e = jnp.exp(s - m)
    pr = e / (jnp.sum(e, axis=-1, keepdims=True) + jnp.exp(sink - m))
    o = jnp.einsum('bnhgqk,bnkhd->bnqhgd', pr.astype(v.dtype), vb)
    return o.reshape(B, S, HQ, hd)


def setup_inputs(seed: int = 0) -> dict:
    key = jax.random.key(seed)
    ks = jax.random.split(key, 24)

    def dense(k, fan_in, fan_out):
        return jax.random.normal(k, (DEPTH, fan_in, fan_out), jnp.float32) * fan_in ** -0.5

    def gain(k, n):
        return 1.0 + 0.05 * jax.random.normal(k, (DEPTH, n), jnp.float32)

    return {
        "x": jax.random.normal(ks[0], (BATCH, SEQ, D_MODEL), jnp.float32),
        "p": jax.random.normal(ks[1], (DEPTH, BATCH, SEQ, PLE_DIM), jnp.float32),
        "g_mix_pre": gain(ks[2], D_MODEL),
        "w_in": dense(ks[3], D_MODEL, IN_COLS),
        "g_q_a": gain(ks[4], Q_LORA),
        "w_q_b": dense(ks[5], Q_LORA, MLA_HEADS * (MLA_NOPE + MLA_ROPE)),
        "g_kv_a": gain(ks[6], KV_LORA),
        "w_kv_b": dense(ks[7], KV_LORA, MLA_HEADS * (MLA_NOPE + MLA_V)),
        "sinks": 0.5 * jax.random.normal(ks[8], (DEPTH, SWA_Q_HEADS), jnp.float32),
        "w_mla_up": dense(ks[9], MLA_HEADS * MLA_V, D_MODEL),
        "w_swa_up": dense(ks[10], SWA_Q_HEADS * SWA_HEAD_DIM, D_MODEL),
        "w_out": dense(ks[11], D_MODEL, D_MODEL),
        "g_mix_post": gain(ks[12], D_MODEL),
        "g_mlp_pre": gain(ks[13], D_MODEL),
        "w_mlp_up": dense(ks[14], D_MODEL, D_FF),
        "w_mlp_down": dense(ks[15], D_FF, D_MODEL),
        "g_mlp_post": gain(ks[16], D_MODEL),
        "w_ple": dense(ks[17], PLE_DIM, D_MODEL),
        "g_ple": gain(ks[18], D_MODEL),
        "w_ple_gate": dense(ks[19], D_MODEL, D_MODEL),
    }


def reference(x, p, g_mix_pre, w_in, g_q_a, w_q_b, g_kv_a, w_kv_b, sinks, w_mla_up, w_swa_up,
              w_out, g_mix_post, g_mlp_pre, w_mlp_up, w_mlp_down, g_mlp_post, w_ple, g_ple,
              w_ple_gate):
    B, S, _ = x.shape
    pos = jnp.arange(S)
    for i in range(DEPTH):
        h = rms_norm(x, g_mix_pre[i])
        z = h @ w_in[i]
        q_a, kv_a, k_r, sq, sk, sv, gates = jnp.split(z, IN_OFFSETS, axis=-1)

        qm = (rms_norm(q_a, g_q_a[i]) @ w_q_b[i]).reshape(B, S, MLA_HEADS, MLA_NOPE + MLA_ROPE)
        q_nope, q_rope = qm[..., :MLA_NOPE], rope(qm[..., MLA_NOPE:], pos)
        kvm = (rms_norm(kv_a, g_kv_a[i]) @ w_kv_b[i]).reshape(B, S, MLA_HEADS, MLA_NOPE + MLA_V)
        k_nope, v_m = kvm[..., :MLA_NOPE], kvm[..., MLA_NOPE:]
        k_rope = jnp.broadcast_to(rope(k_r[:, :, None, :], pos), (B, S, MLA_HEADS, MLA_ROPE))
        q_mla = jnp.concatenate([q_nope, q_rope], axis=-1)
        k_mla = jnp.concatenate([k_nope, k_rope], axis=-1)
        o_mla = mla_attention(q_mla, k_mla, v_m).reshape(B, S, MLA_HEADS * MLA_V)

        q_s = rope(sq.reshape(B, S, SWA_Q_HEADS, SWA_HEAD_DIM), pos)
        k_s = rope(sk.reshape(B, S, SWA_KV_HEADS, SWA_HEAD_DIM), pos)
        v_s = sv.reshape(B, S, SWA_KV_HEADS, SWA_HEAD_DIM)
        o_swa = swa_attention(q_s, k_s, v_s, sinks[i]).reshape(B, S, SWA_Q_HEADS * SWA_HEAD_DIM)

        g_a, g_b = gates[..., :D_MODEL], gates[..., D_MODEL:]
        y = jax.nn.sigmoid(g_a) * (o_mla @ w_mla_up[i]) + jax.nn.sigmoid(g_b) * (o_swa @ w_swa_up[i])
        x = x + rms_norm(y @ w_out[i], g_mix_post[i])

        h = rms_norm(x, g_mlp_pre[i])
        u = jnp.square(jax.nn.relu(h @ w_mlp_up[i]))
        x = x + rms_norm(u @ w_mlp_down[i], g_mlp_post[i])

        e = rms_norm(p[i] @ w_ple[i], g_ple[i])
        x = x + jax.nn.sigmoid(x @ w_ple_gate[i]) * e
    return x
```

```python
import math
import os
import numpy as np
from contextlib import ExitStack
import ml_dtypes
import concourse.bass as bass
import concourse.mybir as mybir
from concourse.bass_utils import run_bass_kernel_spmd

F32 = mybir.dt.float32
BF16 = mybir.dt.bfloat16
ALU = mybir.AluOpType
AF = mybir.ActivationFunctionType

ENGS = ["pe", "act", "dve", "pool", "sp"]

S = 4096
D = 1024
NT = 32
NG = 8
SC_MLA = 96 ** -0.5
SC_SWA = 64 ** -0.5
EPS = 1e-6


class Prog:
    def __init__(self, nc, stack):
        self.nc = nc
        self.stack = stack
        self.q = {e: [] for e in ENGS}
        self.cnt = {}
        self.seen = {e: {} for e in ENGS}
        self.sems = {}
        self.lastw = {}
        self.readers = {}
        for e in ENGS:
            self._sem(e)

    def _sem(self, name):
        if name not in self.sems:
            self.sems[name] = self.stack.enter_context(self.nc.semaphore("s_" + name))
            self.cnt[name] = 0
        return self.sems[name]

    def _wait(self, eng, s, v):
        if eng == "pe" and s == "pe":
            return
        if self.seen[eng].get(s, 0) < v:
            self.seen[eng][s] = v
            sem = self.sems[s]
            self.q[eng].append(lambda h, sem=sem, v=v: h.wait_ge(sem, v))

    def _deps(self, eng, reads, writes):
        need = {}

        def add(t):
            if t is None:
                return
            s, v = t
            if need.get(s, 0) < v:
                need[s] = v

        for r in reads:
            add(self.lastw.get(r))
        for w in writes:
            add(self.lastw.get(w))
            for t in self.readers.get(w, ()):
                add(t)
        for s, v in need.items():
            self._wait(eng, s, v)

    def _commit(self, ticket, reads, writes):
        for r in reads:
            lst = self.readers.setdefault(r, [])
            lst[:] = [t for t in lst if t[0] != ticket[0]]
            lst.append(ticket)
        for w in writes:
            self.lastw[w] = ticket
            self.readers[w] = []

    def op(self, eng, fn, reads=(), writes=()):
        reads = list(reads)
        writes = list(writes)
        self._deps(eng, reads, writes)
        self.cnt[eng] += 1
        sem = self.sems[eng]
        self.q[eng].append(lambda h, sem=sem, fn=fn: fn(h).then_inc(sem, 1))
        t = (eng, self.cnt[eng])
        self._commit(t, reads, writes)
        return t

    def dma(self, eng, semname, fn, reads=(), writes=()):
        reads = list(reads)
        writes = list(writes)
        sem = self._sem(semname)
        self._deps(eng, reads, writes)
        self.cnt[semname] += 16
        self.q[eng].append(lambda h, sem=sem, fn=fn: fn(h).then_inc(sem, 16))
        t = (semname, self.cnt[semname])
        self._commit(t, reads, writes)
        return t

    def wait_all(self, eng, keys):
        self._deps(eng, list(keys), [])

    def barrier(self, final=False):
        keep = self.lastw.get("scratch")
        for e in ENGS:
            for s, c in self.cnt.items():
                if c > 0 and (final or s != "d_scr"):
                    self._wait(e, s, c)
        self.lastw.clear()
        self.readers.clear()
        if keep is not None and not final:
            self.lastw["scratch"] = keep

    def emit(self):
        nc = self.nc
        q = self.q
        with nc.Block() as block:
            @block.tensor
            def _(h):
                for f in q["pe"]:
                    f(h)

            @block.scalar
            def _(h):
                for f in q["act"]:
                    f(h)

            @block.vector
            def _(h):
                for f in q["dve"]:
                    f(h)

            @block.gpsimd
            def _(h):
                for f in q["pool"]:
                    f(h)

            @block.sync
            def _(h):
                for f in q["sp"]:
                    f(h)


def build(stage=3, dbg=()):
    nc = bass.Bass("TRN2", target_bir_lowering=False)

    def din(name, shape, dt=F32):
        return nc.dram_tensor(name, list(shape), dt, kind="ExternalInput").ap()

    x = din("x", [S, D])
    p_in = din("p", [S, 256])
    w_in = din("w_in", [1024, 3744])
    w_q_b = din("w_q_b", [256, 1536])
    w_kv_b = din("w_kv_b", [128, 2048])
    w_mla_up = din("w_mla_up", [1024, 1024])
    w_swa_up = din("w_swa_up", [1024, 1024])
    w_out = din("w_out", [1024, 1024])
    w_mlp_up = din("w_mlp_up", [1024, 4096])
    w_mlp_down = din("w_mlp_down", [4096, 1024])
    w_ple = din("w_ple", [256, 1024])
    w_ple_gate = din("w_ple_gate", [1024, 1024])
    gcol = din("gcol", [128, 19])
    grow = din("grow", [128, 3, 1024])
    sinks_b = din("sinks_b", [128, 16])
    ident = din("ident", [128, 128], BF16)
    maskD = din("maskD", [128, 128], BF16)
    maskP = din("maskP", [128, 128], BF16)
    ropeK = din("ropeK", [128, 32, 64])
    ropeS = din("ropeS", [128, 32, 64])
    ropeQ = din("ropeQ", [2, 128, S])
    out = nc.dram_tensor("out", [S, D], F32, kind="ExternalOutput").ap()
    dbg_t = {}
    for name, shape in dbg:
        dbg_t[name] = nc.dram_tensor("dbg_" + name, list(shape), F32, kind="ExternalOutput").ap()

    sc_in = nc.dram_tensor("sc_in", [1024, 3744], BF16).ap()
    sc_upA = nc.dram_tensor("sc_upA", [1024, 1024], BF16).ap()
    sc_upB = nc.dram_tensor("sc_upB", [1024, 1024], BF16).ap()
    sc_out = nc.dram_tensor("sc_out", [1024, 1024], BF16).ap()
    sc_w1 = nc.dram_tensor("sc_w1", [1024, 4096], BF16).ap()
    sc_w2 = nc.dram_tensor("sc_w2", [4096, 1024], BF16).ap()
    sc_pg = nc.dram_tensor("sc_pg", [1024, 1024], BF16).ap()

    def sbuf(st, n, s, d):
        return st.enter_context(nc.sbuf_tensor(n, list(s), d))

    def psum(st, n, s, d):
        return st.enter_context(nc.psum_tensor(n, list(s), d))

    with ExitStack() as st0:
        P = Prog(nc, st0)

        def mm(out_, lhsT, rhs, start=True, stop=True, reads=(), writes=()):
            return P.op("pe", lambda h: h.matmul(out_, lhsT=lhsT, rhs=rhs, start=start, stop=stop), reads, writes)

        def tr(out_, in_, idn, reads=(), writes=()):
            return P.op("pe", lambda h: h.transpose(out=out_, in_=in_, identity=idn), reads, writes)

        def act(out_, in_, func, reads=(), writes=(), eng="act", **kw):
            return P.op(eng, lambda h: h.activation(out=out_, in_=in_, func=func, **kw), reads, writes)

        def tcopy(eng, out_, in_, reads=(), writes=()):
            if eng == "act":
                return P.op(eng, lambda h: h.copy(out=out_, in_=in_), reads, writes)
            return P.op(eng, lambda h: h.tensor_copy(out=out_, in_=in_), reads, writes)

        def tt(eng, out_, in0, in1, op, reads=(), writes=()):
            return P.op(eng, lambda h: h.tensor_tensor(out=out_, in0=in0, in1=in1, op=op), reads, writes)

        def ts(eng, out_, in0, s1, s2, op0, op1=None, reads=(), writes=(), **kw):
            if op1 is None:
                return P.op(eng, lambda h: h.tensor_scalar(out=out_, in0=in0, scalar1=s1, scalar2=None, op0=op0, **kw), reads, writes)
            return P.op(eng, lambda h: h.tensor_scalar(out=out_, in0=in0, scalar1=s1, scalar2=s2, op0=op0, op1=op1, **kw), reads, writes)

        def stt(eng, out_, in0, scalar, in1, op0, op1, reads=(), writes=()):
            return P.op(eng, lambda h: h.scalar_tensor_tensor(out=out_, in0=in0, scalar=scalar, in1=in1, op0=op0, op1=op1), reads, writes)

        def memset(eng, ap, val, writes=()):
            return P.op(eng, lambda h: h.memset(ap, val), (), writes)

        uniq = [0]

        def dma(eng, sem, out_, in_, reads=(), writes=()):
            if sem in ("d_c", "d_w2"):
                uniq[0] += 1
                sem = f"d_u{uniq[0]}"
            return P.dma(eng, sem, lambda h: h.dma_start(out=out_, in_=in_), reads, writes)

        def dump(name, src_ap, key):
            if name in dbg_t:
                dma("sp", "d_dbg", dbg_t[name], src_ap, reads=[key], writes=["dbg_" + name])

        o_mlaT = sbuf(st0, "o_mlaT", [128, 8, S], BF16)
        identb = sbuf(st0, "identb", [128, 128], BF16)
        maskD_sb = sbuf(st0, "maskD_sb", [128, 128], BF16)
        maskP_sb = sbuf(st0, "maskP_sb", [128, 128], BF16)
        gcol_sb = sbuf(st0, "gcol_sb", [128, 19], F32)
        epsb = sbuf(st0, "epsb", [128, 1], F32)
        ones32 = sbuf(st0, "ones32", [128, 128], F32)

        dma("sp", "d_c", identb[:], ident, writes=["identb"])
        dma("sp", "d_c", maskD_sb[:], maskD, writes=["maskD"])
        dma("sp", "d_c", maskP_sb[:], maskP, writes=["maskP"])
        dma("sp", "d_c", gcol_sb[:], gcol, writes=["gcol"])
        memset("pool", epsb[:], EPS, writes=["epsb"])
        memset("pool", ones32[:], 1.0, writes=["ones32"])

        def rstd_from_ss(ss_ap, out_ap, lnv_ap, inv_n, rkey, wkey, lkey):
            act(lnv_ap, ss_ap, AF.Ln, reads=[rkey, "epsb"], writes=[lkey], scale=inv_n, bias=epsb[:, 0:1])
            act(out_ap, lnv_ap, AF.Exp, reads=[lkey], writes=[wkey], scale=-0.5)

        cast_jobs = []

        def cast_rows(dst, src, rows):
            for r0 in range(0, rows, 128):
                cast_jobs.append((dst[r0:r0 + 128, :], src[r0:r0 + 128, :]))

        def cast_emit(n=1):
            for _ in range(n):
                if cast_jobs:
                    d_, s_ = cast_jobs.pop(0)
                    dma("pool", "d_scr", d_, s_, writes=["scratch"])

        with ExitStack() as st12:
            qaT = sbuf(st12, "qaT", [128, 2, S], BF16)
            kvaT = sbuf(st12, "kvaT", [128, S], BF16)
            kT = [sbuf(st12, f"kT{b}", [96, S], BF16) for b in range(2)]
            wqb = sbuf(st12, "wqb", [128, 2, 1536], BF16)
            wqrot = sbuf(st12, "wqrot", [128, 2, 16, 96], BF16)
            wkvb = sbuf(st12, "wkvb", [128, 2048], BF16)

            with ExitStack() as st1:
                NB1 = 6
                w_g0 = sbuf(st1, "w_g0", [128, 8, 448], BF16)
                ropeK_sb = sbuf(st1, "ropeK_sb", [128, 32, 64], F32)
                xt = [sbuf(st1, f"p1xt{i}", [128, D], F32) for i in range(NB1)]
                junk = [sbuf(st1, f"p1junk{i}", [128, D], BF16) for i in range(NB1)]
                st_sb = [sbuf(st1, f"p1st_sb{i}", [128, 8], F32) for i in range(NB1)]
                xn = [sbuf(st1, f"p1xn{i}", [128, D], BF16) for i in range(NB1)]
                hT = [sbuf(st1, f"p1hT{i}", [128, 8, 128], BF16) for i in range(NB1)]
                qan = [sbuf(st1, f"p1qan{i}", [128, 256], BF16) for i in range(NB1)]
                kvan = [sbuf(st1, f"p1kvan{i}", [128, 128], BF16) for i in range(NB1)]
                krm = [sbuf(st1, f"p1krm{i}", [128, 128], BF16) for i in range(NB1)]
                kt12 = [sbuf(st1, f"p1kt12{i}", [128, 64], F32) for i in range(NB1)]
                pT = [psum(st1, f"pT{i}", [128, 8, 128], BF16) for i in range(2)]
                pG0 = [psum(st1, f"pG0{i}", [128, 512], F32) for i in range(3)]
                pTB = [psum(st1, f"pTB{i}", [128, 8, 128], BF16) for i in range(2)]

                dma("pool", "d_w1", w_g0[:, :, 0:416], w_in[:, 0:416].rearrange("(kc p) c -> p kc c", p=128), writes=["w_g0"])
                dma("sp", "d_c", ropeK_sb[:], ropeK, writes=["ropeK"])
                ts("pool", w_g0[:, :, 416:432], w_g0[:, :, 400:416], -1.0, None, ALU.mult, reads=["w_g0"], writes=["w_g0r"])
                tcopy("pool", w_g0[:, :, 432:448], w_g0[:, :, 384:400], reads=["w_g0"], writes=["w_g0r2"])
                for i in range(NB1):
                    memset("pool", krm[i][:], 0.0, writes=[f"krm{i}"])
                dma("pool", "d_w2", wqb[:], w_q_b.rearrange("(kc p) c -> p kc c", p=128), writes=["wqb"])
                dma("pool", "d_w2", wkvb[:], w_kv_b, writes=["wkvb"])
                memset("pool", wqrot[:], 0.0, writes=["wqrot"])
                wq4 = wqb[:].rearrange("p k (h c) -> p k h c", c=96)
                ts("pool", wqrot[:, :, :, 64:80], wq4[:, :, :, 80:96], -1.0, None, ALU.mult, reads=["wqb", "wqrot"], writes=["wqrot"])
                tcopy("pool", wqrot[:, :, :, 80:96], wq4[:, :, :, 64:80], reads=["wqb", "wqrot"], writes=["wqrot"])

                if stage >= 3:
                    cast_rows(sc_in, w_in, 1024)
                    cast_rows(sc_upA, w_mla_up, 1024)
                    cast_rows(sc_upB, w_swa_up, 1024)
                    cast_rows(sc_out, w_out, 1024)
                    cast_rows(sc_w1, w_mlp_up, 1024)
                    cast_rows(sc_w2, w_mlp_down, 4096)
                    cast_rows(sc_pg, w_ple_gate, 1024)

                def p1_stage(t, stage_):
                    i = t % NB1
                    K_ = lambda n: f"{n}{i}"
                    xb, jk, sb_, xnb, hTb = xt[i], junk[i], st_sb[i], xn[i], hT[i]
                    pTt, pTk = pT[t % 2], f"pT{t % 2}"
                    pG, pGk = pG0[t % 3], f"pG0{t % 3}"
                    pTBt, pTBk = pTB[t % 2], f"pTB{t % 2}"
                    if stage_ == 0:
                        dma("sp", f"d_x{i}", xb[:], x[t * 128:(t + 1) * 128, :], writes=[K_("xt")])
                        act(jk[:], xb[:], AF.Square, reads=[K_("xt")], writes=[K_("junk"), K_("ss")], accum_out=sb_[:, 0:1])
                        rstd_from_ss(sb_[:, 0:1], sb_[:, 2:3], sb_[:, 1:2], 1.0 / D, K_("ss"), K_("rstd"), K_("lnv"))
                        ts("dve", xnb[:], xb[:], sb_[:, 2:3], None, ALU.mult, reads=[K_("xt"), K_("rstd")], writes=[K_("xn")])
                    elif stage_ == 1:
                        for kc in range(8):
                            tr(pTt[:, kc, :], xnb[:, kc * 128:(kc + 1) * 128], identb[:], reads=[K_("xn"), "identb"], writes=[pTk])
                        tt("dve", hTb[:], pTt[:], gcol_sb[:, 0:8].unsqueeze(2).broadcast_to([128, 8, 128]), ALU.mult,
                           reads=[pTk, "gcol"], writes=[K_("hT")])
                    elif stage_ == 2:
                        for kc in range(8):
                            mm(pG[:, 0:448], hTb[:, kc, :], w_g0[:, kc, :], start=(kc == 0), stop=(kc == 7),
                               reads=[K_("hT"), "w_g0", "w_g0r", "w_g0r2"], writes=[pGk])
                        act(jk[:, 0:256], pG[:, 0:256], AF.Square, reads=[pGk], writes=[K_("junk"), K_("ss2a")], accum_out=sb_[:, 3:4])
                        act(jk[:, 256:384], pG[:, 256:384], AF.Square, reads=[pGk], writes=[K_("junk"), K_("ss2b")], accum_out=sb_[:, 4:5])
                        rstd_from_ss(sb_[:, 3:4], sb_[:, 5:6], sb_[:, 6:7], 1.0 / 256, K_("ss2a"), K_("rs2a"), K_("lnv2a"))
                        rstd_from_ss(sb_[:, 4:5], sb_[:, 7:8], sb_[:, 6:7], 1.0 / 128, K_("ss2b"), K_("rs2b"), K_("lnv2a"))
                    elif stage_ == 3:
                        ts("dve", qan[i][:], pG[:, 0:256], sb_[:, 5:6], None, ALU.mult, reads=[pGk, K_("rs2a")], writes=[K_("qan")])
                        ts("dve", kvan[i][:], pG[:, 256:384], sb_[:, 7:8], None, ALU.mult, reads=[pGk, K_("rs2b")], writes=[K_("kvan")])
                        tt("dve", kt12[i][:], pG[:, 384:448], ropeK_sb[:, t, :], ALU.mult, reads=[pGk, "ropeK"], writes=[K_("kt12")])
                        tt("dve", krm[i][:, 64:96], kt12[i][:, 0:32], kt12[i][:, 32:64], ALU.add, reads=[K_("kt12")], writes=[K_("krm")])
                    else:
                        tr(pTBt[:, 0, :], qan[i][:, 0:128], identb[:], reads=[K_("qan"), "identb"], writes=[pTBk])
                        tr(pTBt[:, 1, :], qan[i][:, 128:256], identb[:], reads=[K_("qan"), "identb"], writes=[pTBk])
                        tr(pTBt[:, 2, :], kvan[i][:], identb[:], reads=[K_("kvan"), "identb"], writes=[pTBk])
                        tr(pTBt[:, 3, :], krm[i][:], identb[:], reads=[K_("krm"), "identb"], writes=[pTBk])
                        tsl = slice(t * 128, (t + 1) * 128)
                        tt("dve", qaT[:, :, tsl], pTBt[:, 0:2, :], gcol_sb[:, 16:18].unsqueeze(2).broadcast_to([128, 2, 128]), ALU.mult,
                           reads=[pTBk, "gcol"], writes=["qaT"])
                        ts("dve", kvaT[:, tsl], pTBt[:, 2, :], gcol_sb[:, 18:19], None, ALU.mult, reads=[pTBk, "gcol"], writes=["kvaT"])
                        tcopy("dve", kT[0][64:96, tsl], pTBt[64:96, 3, :], reads=[pTBk], writes=["kT0r"])
                        tcopy("dve", kT[1][64:96, tsl], pTBt[64:96, 3, :], reads=[pTBk], writes=["kT1r"])

                NST = 5
                for it in range(NT + NST - 1):
                    for k in range(NST):
                        if 0 <= it - k < NT:
                            p1_stage(it - k, k)
                P.barrier()
                if "qaT" in dbg_t:
                    for kc in range(2):
                        for c in range(4):
                            tcopy("dve", xt[0][:], qaT[:, kc, c * 1024:(c + 1) * 1024], reads=["qaT"], writes=["xt0"])
                            dma("sp", "d_dbg", dbg_t["qaT"][kc, :, c * 1024:(c + 1) * 1024], xt[0][:], reads=["xt0"], writes=["dbg"])
                    for c in range(4):
                        tcopy("dve", xt[0][0:96, :], kT[0][:, c * 1024:(c + 1) * 1024], reads=["qaT"], writes=["xt0"])
                        dma("sp", "d_dbg", dbg_t["kT"][:, c * 1024:(c + 1) * 1024], xt[0][0:96, :], reads=["xt0"], writes=["dbg"])
                    P.barrier()

            if stage >= 2:
                with ExitStack() as st2:
                    cosT = sbuf(st2, "cosT", [128, S], F32)
                    sinT = sbuf(st2, "sinT", [128, S], F32)
                    Vp = [sbuf(st2, f"Vp{b}", [128, 32, 128], BF16) for b in range(2)]
                    qT = [sbuf(st2, f"qT{b}", [96, S], BF16) for b in range(2)]
                    NPT = 6
                    PT = [sbuf(st2, f"PT{i}", [128, 512], BF16) for i in range(NPT)]
                    rt1 = sbuf(st2, "rt1", [128, 512], F32)
                    rt2 = sbuf(st2, "rt2", [128, 512], F32)
                    recrow = [sbuf(st2, f"recrow{i}", [128, 512], F32) for i in range(2)]

                    NPS = 3
                    pS = [psum(st2, f"pS{i}", [128, 512], F32) for i in range(NPS)]
                    NPO = 3
                    pO = [psum(st2, f"pO{i}", [128, 512], F32) for i in range(NPO)]
                    pgen = [psum(st2, f"pgen{i}", [128, 512], F32) for i in range(2)]

                    dma("sp", "d_c", cosT[:], ropeQ[0], writes=["cosT"])
                    dma("sp", "d_c", sinT[:], ropeQ[1], writes=["sinT"])
                    for b in range(2):
                        memset("pool", Vp[b][:], 1.0, writes=[f"Vp{b}"])

                    gen_i = [0]

                    def gen_steps(h):
                        b = h % 2
                        steps = []

                        def k_step(c):
                            pg = pgen[gen_i[0] % 2]
                            pk = f"pgen{gen_i[0] % 2}"
                            gen_i[0] += 1
                            mm(pg[0:64, :], wkvb[:, h * 128:h * 128 + 64], kvaT[:, c * 512:(c + 1) * 512],
                               reads=["wkvb"], writes=[pk])
                            tcopy("dve", kT[b][0:64, c * 512:(c + 1) * 512], pg[0:64, :], reads=[pk], writes=[f"kT{b}"])

                        def v_step(tg):
                            pg = pgen[gen_i[0] % 2]
                            pk = f"pgen{gen_i[0] % 2}"
                            gen_i[0] += 1
                            for i in range(8):
                                tl = tg * 8 + i
                                mm(pg[:, i * 64:(i + 1) * 64], kvaT[:, tl * 128:(tl + 1) * 128], wkvb[:, h * 128 + 64:h * 128 + 128],
                                   reads=["wkvb"], writes=[pk])
                            tcopy("dve", Vp[b][:, tg * 8:(tg + 1) * 8, 0:64], pg[:].rearrange("p (i c) -> p i c", c=64),
                                  reads=[pk], writes=[f"Vp{b}"])

                        def q_step(c):
                            cs = slice(c * 512, (c + 1) * 512)
                            pq = pgen[gen_i[0] % 2]
                            pqk = f"pgen{gen_i[0] % 2}"
                            gen_i[0] += 1
                            pr = pgen[gen_i[0] % 2]
                            prk = f"pgen{gen_i[0] % 2}"
                            gen_i[0] += 1
                            for kc in range(2):
                                mm(pq[0:96, :], wqb[:, kc, h * 96:(h + 1) * 96], qaT[:, kc, cs], start=(kc == 0), stop=(kc == 1),
                                   reads=["wqb"], writes=[pqk])
                            for kc in range(2):
                                mm(pr[0:96, :], wqrot[:, kc, h, :], qaT[:, kc, cs], start=(kc == 0), stop=(kc == 1),
                                   reads=["wqrot"], writes=[prk])
                            tcopy("dve", qT[b][0:64, cs], pq[0:64, :], reads=[pqk], writes=[f"qT{b}"])
                            tt("dve", rt1[64:96, :], pq[64:96, :], cosT[64:96, cs], ALU.mult, reads=[pqk, "cosT"], writes=["rt1"])
                            tt("dve", rt2[64:96, :], pr[64:96, :], sinT[64:96, cs], ALU.mult, reads=[prk, "sinT"], writes=["rt2"])
                            tt("pool", qT[b][64:96, cs], rt1[64:96, :], rt2[64:96, :], ALU.add, reads=["rt1", "rt2"], writes=[f"qT{b}"])

                        for c in range(8):
                            steps.append(lambda c=c: k_step(c))
                            if c % 2 == 0:
                                steps.append(lambda tg=c // 2: v_step(tg))
                            steps.append(lambda c=c: q_step(c))
                        return steps

                    def gen_head(h):
                        for f in gen_steps(h):
                            f()

                    units = []
                    for h in range(16):
                        for c in range(8):
                            nkb = 4 * c + 4
                            for kb in range(nkb):
                                q0 = (kb - 4 * c) * 128 if kb >= 4 * c else 0
                                units.append((h, c, kb, q0, kb == 0, kb == nkb - 1, nkb))
                    LOOK = 2
                    pending = []
                    infos = {}

                    def qk(j):
                        h, c, kb, q0, first, last, nkb = units[j]
                        b = h % 2
                        ps_, psk = pS[j % NPS], f"pS{j % NPS}"
                        pt_, ptk = PT[j % NPT], f"PT{j % NPT}"
                        diag = kb >= 4 * c
                        kblk = kT[b][0:96, kb * 128:(kb + 1) * 128]
                        rk = [f"kT{b}", f"kT{b}r", f"qT{b}"]
                        mm(ps_[:, q0:512], kblk, qT[b][0:96, c * 512 + q0:(c + 1) * 512],
                           start=True, stop=(not diag), reads=rk, writes=[psk])
                        if diag:
                            mm(ps_[:, q0:q0 + 128], identb[:], maskD_sb[:], start=False, stop=True,
                               reads=["identb", "maskD"], writes=[psk])
                        act(pt_[:, q0:512], ps_[:, q0:512], AF.Exp, reads=[psk], writes=[ptk], scale=SC_MLA)
                        infos[j] = (pt_, ptk)

                    def pv(j):
                        h, c, kb, q0, first, last, nkb = units[j]
                        b = h % 2
                        ci = h * 8 + c
                        po, pok = pO[ci % NPO], f"pO{ci % NPO}"
                        pt_, ptk = infos.pop(j)
                        mm(po[:, q0:512], Vp[b][:, kb, :], pt_[:, q0:512], start=first, stop=last,
                           reads=[f"Vp{b}", ptk], writes=[pok])
                        if last:
                            rr, rrk = recrow[ci % 2], f"recrow{ci % 2}"
                            P.op("dve", lambda hh, rr=rr, po=po: hh.reciprocal(out=rr[64:128, :], in_=po[64:128, :]), reads=[pok], writes=[rrk])
                            hp, hs = h // 2, (h % 2) * 64
                            tt("dve", o_mlaT[hs:hs + 64, hp, c * 512:(c + 1) * 512], po[0:64, :], rr[64:128, :], ALU.mult,
                               reads=[pok, rrk], writes=["o_mlaT"])
                        return None

                    gen_head(0)
                    gsteps = []
                    NU = len(units)
                    for j in range(NU + LOOK):
                        if j < NU:
                            h, c, kb = units[j][0], units[j][1], units[j][2]
                            if c == 0 and kb == 0:
                                while gsteps:
                                    gsteps.pop(0)()
                                if h + 1 < 16:
                                    gsteps.extend(gen_steps(h + 1))
                            elif gsteps and j % 6 == 0:
                                gsteps.pop(0)()
                            if j % 20 == 10:
                                cast_emit(1)
                            qk(j)
                        if j >= LOOK:
                            f = pv(j - LOOK)
                            if f is not None:
                                jn = j - LOOK + 1
                                n_next = units[jn][6] if jn < NU else 0
                                pending.append((j + max(0, min(6, n_next - 1)), f))
                        while pending and pending[0][0] <= j:
                            pending.pop(0)[1]()
                    while pending:
                        pending.pop(0)[1]()
                    cast_emit(len(cast_jobs))
                    P.barrier()
                    if "o_mlaT" in dbg_t:
                        for j in range(8):
                            for c in range(8):
                                tcopy("dve", rt1[:], o_mlaT[:, j, c * 512:(c + 1) * 512], reads=["o_mlaT"], writes=["rt1"])
                                dma("sp", "d_dbg", dbg_t["o_mlaT"][j, :, c * 512:(c + 1) * 512], rt1[:], reads=["rt1"], writes=["dbg"])
                        P.barrier()
        P.barrier()
        if stage >= 3:
            with ExitStack() as st3:
                x_g = sbuf(st3, "x_g", [128, 4, D], F32)
                hT3 = sbuf(st3, "hT3", [128, 8, 512], BF16)
                bufA = sbuf(st3, "bufA", [128, 8192], BF16)
                swa_qT = bufA[:, 0:4096].rearrange("p (t j q) -> p t j q", j=8, q=128)
                o_swaT = bufA[:, 4096:8192].rearrange("p (j t) -> p j t", t=512)
                x2acc = bufA[:].bitcast(F32).rearrange("p (t c) -> p t c", c=1024)
                yT = sbuf(st3, "yT", [128, 8, 512], BF16)
                uT = sbuf(st3, "uT", [128, 8, 512], BF16)
                NS = 3
                ring = [sbuf(st3, f"ring{i}", [128, 8, 512], BF16) for i in range(NS)]
                grow_sb = sbuf(st3, "grow_sb", [128, 3, 1024], F32)
                wpp = sbuf(st3, "wpp", [128, 2, 1024], BF16)
                swa_kT = sbuf(st3, "swa_kT", [128, 8 * 128], BF16)
                Vs = sbuf(st3, "Vs", [128, 8, 2, 65], BF16)
                xn3s = [sbuf(st3, f"xn3_{i}", [128, D], BF16) for i in range(2)]
                junk3 = sbuf(st3, "junk3", [128, D], BF16)
                ropeS_g = sbuf(st3, "ropeS_g", [128, 4, 64], F32)
                qr_tms = [sbuf(st3, f"qr_tm{i}", [128, 8, 2, 64], BF16) for i in range(2)]
                kr_tms = [sbuf(st3, f"kr_tm{i}", [128, 2, 2, 32], BF16) for i in range(2)]
                PTs = [sbuf(st3, f"PTs{i}", [128, 1024], BF16) for i in range(4)]
                esink = sbuf(st3, "esink", [128, 16], F32)
                dn = sbuf(st3, "dn", [128, 16], F32)
                o_tm = sbuf(st3, "o_tm", [128, 16, 64], BF16)
                one_b = sbuf(st3, "one_b", [128, 1], F32)
                mt = [sbuf(st3, f"mt{i}", [128, 512], F32) for i in range(2)]
                stg = [sbuf(st3, f"stg{i}", [128, D], F32) for i in range(2)]
                stg2 = sbuf(st3, "stg2", [128, D], F32)
                rtmp = [stg2[:, i * 256:(i + 1) * 256].rearrange("p (h c) -> p h c", c=32) for i in range(4)]
                pt_sbs = [sbuf(st3, f"pt_sb{i}", [128, 256], F32) for i in range(2)]
                pn_sb = sbuf(st3, "pn_sb", [128, 256], BF16)
                pT_sb = sbuf(st3, "pT_sb", [128, 2, 512], BF16)
                ss3 = sbuf(st3, "ss3", [128, 8], F32)
                ln3 = sbuf(st3, "ln3", [128, 8], F32)
                rs3 = sbuf(st3, "rs3", [128, 8], F32)
                pT3 = psum(st3, "pT3", [128, 8, 128], BF16)
                pX = psum(st3, "pX", [128, 512], F32)
                pD = [psum(st3, f"pD{i}", [128, 1024], F32) for i in range(3)]

                dma("sp", "d_c", grow_sb[:], grow, writes=["grow"])
                dma("pool", "d_w3", wpp[:], w_ple.rearrange("(kc p) c -> p kc c", p=128), writes=["wpp"])
                dma("sp", "d_c", esink[:], sinks_b, writes=["esink0"])
                act(esink[:], esink[:], AF.Exp, reads=["esink0"], writes=["esink"])
                memset("pool", one_b[:], 1.0, writes=["one_b"])
                memset("pool", Vs[:], 1.0, writes=["Vs"])

                def wsrc(sc, r0, c0, ncols):
                    return (sc[r0:r0 + 1024, c0:c0 + ncols].rearrange("(kc p) c -> p kc c", p=128), ncols)

                per_group = [wsrc(sc_in, 0, 416, 512), wsrc(sc_in, 0, 928, 512), wsrc(sc_in, 0, 1440, 256)]
                for half in range(2):
                    per_group += [wsrc(sc_in, 0, 1696 + half * 512, 512), wsrc(sc_in, 0, 2720 + half * 512, 512),
                                  wsrc(sc_upA, 0, half * 512, 512), wsrc(sc_upB, 0, half * 512, 512)]
                per_group += [wsrc(sc_out, 0, 0, 512), wsrc(sc_out, 0, 512, 512)]
                for s_ in range(4):
                    per_group += [wsrc(sc_w1, 0, s_ * 1024, 512), wsrc(sc_w1, 0, s_ * 1024 + 512, 512),
                                  wsrc(sc_w2, s_ * 1024, 0, 512), wsrc(sc_w2, s_ * 1024, 512, 512)]
                per_group += [wsrc(sc_pg, 0, 0, 512), wsrc(sc_pg, 0, 512, 512)]
                srcs = per_group * NG
                rs = {"load": 0, "use": 0}

                def ring_load():
                    i = rs["load"]
                    if i >= len(srcs):
                        return
                    slot = i % NS
                    src, ncols = srcs[i]
                    dma("sp", f"d_ring{slot}", ring[slot][:, :, 0:ncols], src, reads=["scratch"], writes=[f"ring{slot}"])
                    rs["load"] += 1

                def ring_get():
                    i = rs["use"]
                    assert i < rs["load"]
                    rs["use"] += 1
                    return ring[i % NS], f"ring{i % NS}"

                def ring_release(n=1):
                    for _ in range(n):
                        ring_load()

                for _ in range(NS):
                    ring_load()

                pd_i = [0]

                def next_pd():
                    i = pd_i[0] % 3
                    pd_i[0] += 1
                    return pD[i], f"pD{i}"

                st_i = [0]

                def rms_tile(src_ap, src_key, col):
                    act(junk3[:], src_ap, AF.Square, reads=[src_key], writes=["junk3", f"ss3_{col}"], accum_out=ss3[:, col:col + 1])
                    rstd_from_ss(ss3[:, col:col + 1], rs3[:, col:col + 1], ln3[:, col:col + 1], 1.0 / D, f"ss3_{col}", f"rs3_{col}", f"ln3_{col}")

                xn_i = [0]

                def to_hT(t, gcols, src_ap, src_key, col):
                    xn3 = xn3s[xn_i[0] % 2]
                    xk = f"xn3_{xn_i[0] % 2}"
                    xn_i[0] += 1
                    if col is not None:
                        ts("dve", xn3[:], src_ap, rs3[:, col:col + 1], None, ALU.mult, reads=[src_key, f"rs3_{col}"], writes=[xk])
                    else:
                        tcopy("dve", xn3[:], src_ap, reads=[src_key], writes=[xk])
                    for kc in range(8):
                        tr(pT3[:, kc, :], xn3[:, kc * 128:(kc + 1) * 128], identb[:], reads=[xk, "identb"], writes=["pT3"])
                    if gcols is not None:
                        tt("dve", hT3[:, :, t * 128:(t + 1) * 128], pT3[:], gcol_sb[:, gcols:gcols + 8].unsqueeze(2).broadcast_to([128, 8, 128]),
                           ALU.mult, reads=["pT3", "gcol"], writes=[f"hT3_{t}"])
                    else:
                        tcopy("dve", hT3[:, :, t * 128:(t + 1) * 128], pT3[:], reads=["pT3"], writes=[f"hT3_{t}"])

                def sigmoid_from(ps_ap, ps_key, tmp_ap, tmp_key, out_ap, out_key):
                    act(tmp_ap, ps_ap, AF.Exp, reads=[ps_key], writes=[tmp_key], scale=-1.0)
                    act(tmp_ap, tmp_ap, AF.Ln, reads=[tmp_key, "one_b"], writes=[tmp_key], bias=one_b[:, 0:1], scale=1.0)
                    act(out_ap, tmp_ap, AF.Exp, reads=[tmp_key], writes=[out_key], scale=-1.0)

                def s1_tile(g_, t):
                    T = 4 * g_ + t
                    dma("sp", f"d_xg{t}", x_g[:, t, :], x[T * 128:(T + 1) * 128, :], writes=[f"x_g{t}"])
                    rms_tile(x_g[:, t, :], f"x_g{t}", t)
                    to_hT(t, 0, x_g[:, t, :], f"x_g{t}", t)

                for g in range(NG):
                    gs = slice(g * 512, (g + 1) * 512)
                    dma("sp", "d_rs", ropeS_g[:], ropeS[:, 4 * g:4 * g + 4, :], writes=["ropeS_g"])
                    if g == 0:
                        for t in range(4):
                            s1_tile(0, t)
                    wlo, wlok = ring_get()
                    whi, whik = ring_get()
                    wkv, wkvk = ring_get()
                    def s2_A(t):
                        T = 4 * g + t
                        qr_tm, qrk = qr_tms[t % 2], f"qr_tm{t % 2}"
                        kr_tm, krk = kr_tms[t % 2], f"kr_tm{t % 2}"
                        tsl = slice(t * 128, (t + 1) * 128)
                        cosb = ropeS_g[:, t, 0:32]
                        sinb = ropeS_g[:, t, 32:64]
                        for hi_, (wr, wrk) in enumerate(((wlo, wlok), (whi, whik))):
                            pd, pdk = next_pd()
                            for kc in range(8):
                                mm(pd[:, 0:512], hT3[:, kc, tsl], wr[:, kc, 0:512], start=(kc == 0), stop=(kc == 7),
                                   reads=[f"hT3_{t}", wrk], writes=[pdk])
                            q4 = pd[:, 0:512].rearrange("p (h two c) -> p h two c", two=2, c=32)
                            cb = cosb.unsqueeze(1).broadcast_to([128, 8, 32])
                            sb_ = sinb.unsqueeze(1).broadcast_to([128, 8, 32])
                            tt("dve", rtmp[0], q4[:, :, 0, :], cb, ALU.mult, reads=[pdk, "ropeS_g"], writes=["rtmp0"])
                            tt("dve", rtmp[1], q4[:, :, 1, :], sb_, ALU.mult, reads=[pdk, "ropeS_g"], writes=["rtmp1"])
                            tt("dve", rtmp[2], q4[:, :, 1, :], cb, ALU.mult, reads=[pdk, "ropeS_g"], writes=["rtmp2"])
                            tt("dve", rtmp[3], q4[:, :, 0, :], sb_, ALU.mult, reads=[pdk, "ropeS_g"], writes=["rtmp3"])
                            tt("pool", qr_tm[:, :, hi_, 0:32], rtmp[0], rtmp[1], ALU.subtract, reads=["rtmp0", "rtmp1"], writes=[qrk])
                            tt("pool", qr_tm[:, :, hi_, 32:64], rtmp[2], rtmp[3], ALU.add, reads=["rtmp2", "rtmp3"], writes=[qrk])
                        pd, pdk = next_pd()
                        for kc in range(8):
                            mm(pd[:, 0:256], hT3[:, kc, tsl], wkv[:, kc, 0:256], start=(kc == 0), stop=(kc == 7),
                               reads=[f"hT3_{t}", wkvk], writes=[pdk])
                        k4 = pd[:, 0:128].rearrange("p (h two c) -> p h two c", two=2, c=32)
                        cb2 = cosb.unsqueeze(1).broadcast_to([128, 2, 32])
                        sb2 = sinb.unsqueeze(1).broadcast_to([128, 2, 32])
                        tt("dve", rtmp[0][:, 0:2, :], k4[:, :, 0, :], cb2, ALU.mult, reads=[pdk, "ropeS_g"], writes=["rtmp0"])
                        tt("dve", rtmp[1][:, 0:2, :], k4[:, :, 1, :], sb2, ALU.mult, reads=[pdk, "ropeS_g"], writes=["rtmp1"])
                        tt("dve", rtmp[2][:, 0:2, :], k4[:, :, 1, :], cb2, ALU.mult, reads=[pdk, "ropeS_g"], writes=["rtmp2"])
                        tt("dve", rtmp[3][:, 0:2, :], k4[:, :, 0, :], sb2, ALU.mult, reads=[pdk, "ropeS_g"], writes=["rtmp3"])
                        tt("pool", kr_tm[:, :, 0, :], rtmp[0][:, 0:2, :], rtmp[1][:, 0:2, :], ALU.subtract, reads=["rtmp0", "rtmp1"], writes=[krk])
                        tt("pool", kr_tm[:, :, 1, :], rtmp[2][:, 0:2, :], rtmp[3][:, 0:2, :], ALU.add, reads=["rtmp2", "rtmp3"], writes=[krk])
                        slot = T % 8
                        tcopy("dve", Vs[:, slot, :, 0:64], pd[:, 128:256].rearrange("p (h c) -> p h c", c=64), reads=[pdk], writes=["Vs"])
                    def s2_B(t):
                        T = 4 * g + t
                        slot = T % 8
                        qr_tm, qrk = qr_tms[t % 2], f"qr_tm{t % 2}"
                        kr_tm, krk = kr_tms[t % 2], f"kr_tm{t % 2}"
                        for i in range(8):
                            tr(pT3[:, i, :], qr_tm[:, i, :, :], identb[:], reads=[qrk, "identb"], writes=["pT3"])
                        tcopy("dve", swa_qT[:, t, :, :], pT3[:], reads=["pT3"], writes=["bufA_q"])
                        tr(pT3[:, 0, :], kr_tm[:], identb[:], reads=[krk, "identb"], writes=["pT3"])
                        tcopy("dve", swa_kT[:, slot * 128:(slot + 1) * 128], pT3[:, 0, :], reads=["pT3"], writes=["swa_kT"])
                    s2_A(0)
                    for t in range(4):
                        if t + 1 < 4:
                            s2_A(t + 1)
                        s2_B(t)
                    ring_release(3)
                    def oslot(hh):
                        if hh < 7:
                            return pD[2], "pD2", hh * 65
                        if hh < 14:
                            return pD[2], "pD2", 512 + (hh - 7) * 65
                        return pX, "pX", (hh - 14) * 65

                    def ulist_of(t):
                        n = 4 * g + t
                        return ([(n - 1, maskP_sb, "maskP")] if n > 0 else []) + [(n, maskD_sb, "maskD")]

                    def s3_A(t, kvh):
                        ks = slice(kvh * 64, (kvh + 1) * 64)
                        for ui, (kb, msk, mskk) in enumerate(ulist_of(t)):
                            pd, pdk = pD[ui], f"pD{ui}"
                            sl = kb % 8
                            for half in range(2):
                                mm(pd[:, half * 512:(half + 1) * 512], swa_kT[ks, sl * 128:(sl + 1) * 128],
                                   swa_qT[ks, t, 4 * half:4 * half + 4, :], start=True, stop=False,
                                   reads=["swa_kT", "bufA_q"], writes=[pdk])
                                for i in range(4 * half, 4 * half + 4):
                                    mm(pd[:, i * 128:(i + 1) * 128], identb[:], msk[:], start=False, stop=(i % 4 == 3),
                                       reads=["identb", mskk], writes=[pdk])
                            ptile, ptk = PTs[kvh * 2 + ui], f"PTs{kvh * 2 + ui}"
                            act(ptile[:], pd[:], AF.Exp, reads=[pdk], writes=[ptk], scale=SC_SWA)

                    def s3_B(t, kvh):
                        ul = ulist_of(t)
                        for i in range(8):
                            hh = kvh * 8 + i
                            ot, otk, oc = oslot(hh)
                            for ui, (kb, msk, mskk) in enumerate(ul):
                                mm(ot[:, oc:oc + 65], PTs[kvh * 2 + ui][:, i * 128:(i + 1) * 128], Vs[:, kb % 8, kvh, :],
                                   start=(ui == 0), stop=(ui == len(ul) - 1), reads=[f"PTs{kvh * 2 + ui}", "Vs"], writes=[otk])

                    def s3_C(t):
                        tsl = slice(t * 128, (t + 1) * 128)
                        grp = [(pD[2], "pD2", 0, 0, 7), (pD[2], "pD2", 512, 7, 7), (pX, "pX", 0, 14, 2)]
                        for (ot, otk, base, h0, nh) in grp:
                            V3 = ot[:, base:base + nh * 65].rearrange("p (h c) -> p h c", c=65)
                            tt("dve", dn[:, h0:h0 + nh], V3[:, :, 64], esink[:, h0:h0 + nh], ALU.add, reads=[otk, "esink"], writes=["dn"])
                        P.op("dve", lambda hh_: hh_.reciprocal(out=dn[:], in_=dn[:]), reads=["dn"], writes=["dn"])
                        for (ot, otk, base, h0, nh) in grp:
                            V3 = ot[:, base:base + nh * 65].rearrange("p (h c) -> p h c", c=65)
                            tt("dve", o_tm[:, h0:h0 + nh, :], V3[:, :, 0:64], dn[:, h0:h0 + nh].unsqueeze(2).broadcast_to([128, nh, 64]), ALU.mult,
                               reads=[otk, "dn"], writes=["o_tm"])
                        for j in range(8):
                            tr(pT3[:, j, :], o_tm[:, 2 * j:2 * j + 2, :], identb[:], reads=["o_tm", "identb"], writes=["pT3"])
                        tcopy("dve", o_swaT[:, :, tsl], pT3[:], reads=["pT3"], writes=["bufA_o"])

                    s3_A(0, 0)
                    s3_A(0, 1)
                    for t in range(4):
                        s3_B(t, 0)
                        s3_B(t, 1)
                        if t + 1 < 4:
                            s3_A(t + 1, 0)
                            s3_A(t + 1, 1)
                        s3_C(t)
                    sA = uT[:, 0:4, :]
                    sB = uT[:, 4:8, :]
                    for half in range(2):
                        for (sdst, sk_) in ((sA, "uTa"), (sB, "uTb")):
                            wr, wrk = ring_get()
                            for jj in range(4):
                                pd, pdk = next_pd()
                                for kc in range(8):
                                    mm(pd[:, 0:512], wr[:, kc, jj * 128:(jj + 1) * 128], hT3[:, kc, :], start=(kc == 0), stop=(kc == 7),
                                       reads=[wrk, "hT3_0", "hT3_1", "hT3_2", "hT3_3"], writes=[pdk])
                                m_ = mt[jj % 2]
                                sigmoid_from(pd[:, 0:512], pdk, m_[:], f"mt{jj % 2}", sdst[:, jj, :], sk_)
                            ring_release(1)
                        wa, wak = ring_get()
                        wb, wbk = ring_get()
                        for jj in range(4):
                            pd, pdk = next_pd()
                            for kc in range(8):
                                mm(pd[:, 0:512], wa[:, kc, jj * 128:(jj + 1) * 128], o_mlaT[:, kc, gs], start=(kc == 0), stop=(kc == 7),
                                   reads=[wak, "o_mlaT"], writes=[pdk])
                            for kc in range(8):
                                mm(pd[:, 512:1024], wb[:, kc, jj * 128:(jj + 1) * 128], o_swaT[:, kc, :], start=(kc == 0), stop=(kc == 7),
                                   reads=[wbk, "bufA_o"], writes=[pdk])
                            tt("dve", mt[0][:], pd[:, 0:512], sA[:, jj, :], ALU.mult, reads=[pdk, "uTa"], writes=["mt0"])
                            tt("dve", mt[1][:], pd[:, 512:1024], sB[:, jj, :], ALU.mult, reads=[pdk, "uTb"], writes=["mt1"])
                            tt("pool", yT[:, half * 4 + jj, :], mt[0][:], mt[1][:], ALU.add, reads=["mt0", "mt1"], writes=["yT"])
                        ring_release(2)
                    w0, w0k = ring_get()
                    w1_, w1k = ring_get()
                    for t in range(4):
                        tsl = slice(t * 128, (t + 1) * 128)
                        pd, pdk = next_pd()
                        for ch, (wr, wrk) in enumerate(((w0, w0k), (w1_, w1k))):
                            for kc in range(8):
                                mm(pd[:, ch * 512:(ch + 1) * 512], yT[:, kc, tsl], wr[:, kc, 0:512], start=(kc == 0), stop=(kc == 7),
                                   reads=["yT", wrk], writes=[pdk])
                        rms_tile(pd[:], pdk, 4 + t)
                        stt("dve", stg2[:], pd[:], rs3[:, 4 + t:5 + t], grow_sb[:, 0, :], ALU.mult, ALU.mult,
                            reads=[pdk, f"rs3_{4 + t}", "grow", "rtmp0", "rtmp1", "rtmp2", "rtmp3"], writes=["rtmp0", "rtmp1", "rtmp2", "rtmp3"])
                        tt("dve", x_g[:, t, :], x_g[:, t, :], stg2[:], ALU.add, reads=[f"x_g{t}", "rtmp0", "rtmp1", "rtmp2", "rtmp3"], writes=[f"x_g{t}"])
                    ring_release(2)
                    for t in range(4):
                        rms_tile(x_g[:, t, :], f"x_g{t}", t)
                        to_hT(t, 8, x_g[:, t, :], f"x_g{t}", t)
                    for s_ in range(4):
                        for cidx in range(2):
                            wr, wrk = ring_get()
                            for jj in range(4):
                                fc = cidx * 4 + jj
                                pd, pdk = next_pd()
                                for kc in range(8):
                                    mm(pd[:, 0:512], wr[:, kc, jj * 128:(jj + 1) * 128], hT3[:, kc, :], start=(kc == 0), stop=(kc == 7),
                                       reads=[wrk, "hT3_0", "hT3_1", "hT3_2", "hT3_3"], writes=[pdk])
                                m_, mk_ = mt[jj % 2], f"mt{jj % 2}"
                                act(m_[:], pd[:, 0:512], AF.Relu, reads=[pdk], writes=[mk_])
                                uk = "uTa" if fc < 4 else "uTb"
                                tt("pool", uT[:, fc, :], m_[:], m_[:], ALU.mult, reads=[mk_], writes=[uk])
                            ring_release(1)
                        for ch in range(2):
                            wr, wrk = ring_get()
                            for t in range(4):
                                tsl = slice(t * 128, (t + 1) * 128)
                                pd, pdk = next_pd()
                                for fc in range(8):
                                    mm(pd[:, 0:512], uT[:, fc, tsl], wr[:, fc, 0:512], start=(fc == 0), stop=(fc == 7),
                                       reads=["uTa", "uTb", wrk], writes=[pdk])
                                ak = "bufA_q" if t < 2 else "bufA_o"
                                dst = x2acc[:, t, ch * 512:(ch + 1) * 512]
                                if s_ == 0:
                                    tcopy("dve", dst, pd[:, 0:512], reads=[pdk], writes=[ak])
                                else:
                                    tt("dve", dst, pd[:, 0:512], dst, ALU.add, reads=[pdk, ak], writes=[ak])
                            ring_release(1)
                    for t in range(4):
                        ak = "bufA_q" if t < 2 else "bufA_o"
                        rms_tile(x2acc[:, t, :], ak, 4 + t)
                        stt("dve", stg2[:], x2acc[:, t, :], rs3[:, 4 + t:5 + t], grow_sb[:, 1, :], ALU.mult, ALU.mult,
                            reads=[ak, f"rs3_{4 + t}", "grow", "rtmp0", "rtmp1", "rtmp2", "rtmp3"], writes=["rtmp0", "rtmp1", "rtmp2", "rtmp3"])
                        tt("dve", x_g[:, t, :], x_g[:, t, :], stg2[:], ALU.add, reads=[f"x_g{t}", "rtmp0", "rtmp1", "rtmp2", "rtmp3"], writes=[f"x_g{t}"])
                    def s8_A(t):
                        T = 4 * g + t
                        to_hT(t, None, x_g[:, t, :], f"x_g{t}", None)
                        ptb, ptk_ = pt_sbs[t % 2], f"pt_sb{t % 2}"
                        dma("sp", f"d_p{t % 2}", ptb[:], p_in[T * 128:(T + 1) * 128, :], writes=[ptk_])
                        tcopy("dve", pn_sb[:], ptb[:], reads=[ptk_], writes=["pn_sb"])
                        for kc in range(2):
                            tr(pT3[:, kc, :], pn_sb[:, kc * 128:(kc + 1) * 128], identb[:], reads=["pn_sb", "identb"], writes=["pT3"])
                        tcopy("dve", pT_sb[:, :, t * 128:(t + 1) * 128], pT3[:, 0:2, :], reads=["pT3"], writes=[f"pT_sb{t}"])

                    w0, w0k = ring_get()
                    w1_, w1k = ring_get()

                    def s8_B(t):
                        T = 4 * g + t
                        tsl = slice(t * 128, (t + 1) * 128)
                        pa, pak = next_pd()
                        pb_, pbk = next_pd()
                        for ch, (wr, wrk) in enumerate(((w0, w0k), (w1_, w1k))):
                            for kc in range(8):
                                mm(pa[:, ch * 512:(ch + 1) * 512], hT3[:, kc, tsl], wr[:, kc, 0:512], start=(kc == 0), stop=(kc == 7),
                                   reads=[f"hT3_{t}", wrk], writes=[pak])
                            for kc in range(2):
                                mm(pb_[:, ch * 512:(ch + 1) * 512], pT_sb[:, kc, tsl], wpp[:, kc, ch * 512:(ch + 1) * 512], start=(kc == 0), stop=(kc == 1),
                                   reads=[f"pT_sb{t}", "wpp"], writes=[pbk])
                        so = stg[st_i[0] % 2]
                        sok = f"stg{st_i[0] % 2}"
                        st_i[0] += 1
                        sigmoid_from(pa[:], pak, so[:], sok, so[:], sok)
                        rms_tile(pb_[:], pbk, 4 + t)
                        stt("dve", stg2[:], pb_[:], rs3[:, 4 + t:5 + t], grow_sb[:, 2, :], ALU.mult, ALU.mult,
                            reads=[pbk, f"rs3_{4 + t}", "grow", "rtmp0", "rtmp1", "rtmp2", "rtmp3"], writes=["rtmp0", "rtmp1", "rtmp2", "rtmp3"])
                        tt("dve", so[:], so[:], stg2[:], ALU.mult, reads=[sok, "rtmp0", "rtmp1", "rtmp2", "rtmp3"], writes=[sok])
                        tt("dve", so[:], so[:], x_g[:, t, :], ALU.add, reads=[sok, f"x_g{t}"], writes=[sok])
                        dma("pool", f"d_out{st_i[0] % 2}", out[T * 128:(T + 1) * 128, :], so[:], reads=[sok], writes=["out"])

                    s8_A(0)
                    for t in range(4):
                        if t + 1 < 4:
                            s8_A(t + 1)
                        s8_B(t)
                        if g + 1 < NG and t >= 1:
                            s1_tile(g + 1, t - 1)
                    if g + 1 < NG:
                        s1_tile(g + 1, 3)
                    ring_release(2)
                P.barrier()
        P.barrier(final=True)
        P.emit()
    return nc


def _rope_tables():
    pos = np.arange(S, dtype=np.float64)
    inv16 = np.exp(-math.log(10000.0) * np.arange(16, dtype=np.float64) * (2.0 / 32))
    ang16 = pos[:, None] * inv16[None, :]
    c16, s16 = np.cos(ang16), np.sin(ang16)
    ck = np.concatenate([c16, c16, s16, s16], axis=1).astype(np.float32)
    ropeK = ck.reshape(NT, 128, 64).transpose(1, 0, 2).copy()
    ropeQ = np.zeros((2, 128, S), np.float32)
    ropeQ[0, 64:96, :] = np.concatenate([c16, c16], axis=1).T.astype(np.float32)
    ropeQ[1, 64:96, :] = np.concatenate([s16, s16], axis=1).T.astype(np.float32)
    inv32 = np.exp(-math.log(10000.0) * np.arange(32, dtype=np.float64) * (2.0 / 64))
    ang32 = pos[:, None] * inv32[None, :]
    cs = np.concatenate([np.cos(ang32), np.sin(ang32)], axis=1).astype(np.float32)
    ropeS = cs.reshape(NT, 128, 64).transpose(1, 0, 2).copy()
    return ropeK, ropeS, ropeQ


def make_in_maps(inputs):
    f = lambda a: np.ascontiguousarray(np.asarray(a, dtype=np.float32))
    x = f(inputs["x"])
    p = f(inputs["p"])[0]
    gmp, gqa, gkv = f(inputs["g_mix_pre"])[0], f(inputs["g_q_a"])[0], f(inputs["g_kv_a"])[0]
    gmlp = f(inputs["g_mlp_pre"])[0]
    gcol = np.concatenate([gmp.reshape(8, 128).T, gmlp.reshape(8, 128).T, gqa.reshape(2, 128).T, gkv.reshape(1, 128).T], axis=1)
    grow = np.stack([np.broadcast_to(f(inputs[k])[0][None, :], (128, 1024)) for k in ("g_mix_post", "g_mlp_post", "g_ple")], axis=1)
    sinks_b = np.broadcast_to(f(inputs["sinks"])[0][None, :], (128, 16))
    ropeK, ropeS, ropeQ = _rope_tables()
    kk = np.arange(128)
    NEGB = np.float32(-30000.0)
    maskD = np.where(kk[:, None] <= kk[None, :], np.float32(0), NEGB).astype(ml_dtypes.bfloat16)
    maskP = np.where(kk[:, None] > kk[None, :], np.float32(0), NEGB).astype(ml_dtypes.bfloat16)
    common = {
        "w_in": f(inputs["w_in"])[0], "w_q_b": f(inputs["w_q_b"])[0], "w_kv_b": f(inputs["w_kv_b"])[0],
        "w_mla_up": f(inputs["w_mla_up"])[0], "w_swa_up": f(inputs["w_swa_up"])[0], "w_out": f(inputs["w_out"])[0],
        "w_mlp_up": f(inputs["w_mlp_up"])[0], "w_mlp_down": f(inputs["w_mlp_down"])[0],
        "w_ple": f(inputs["w_ple"])[0], "w_ple_gate": f(inputs["w_ple_gate"])[0],
        "gcol": np.ascontiguousarray(gcol), "grow": np.ascontiguousarray(grow), "sinks_b": np.ascontiguousarray(sinks_b),
        "ident": np.eye(128, dtype=np.float32).astype(ml_dtypes.bfloat16), "maskD": maskD, "maskP": maskP,
        "ropeK": ropeK, "ropeS": ropeS, "ropeQ": ropeQ,
    }
    return [dict(common, x=np.ascontiguousarray(x[b]), p=np.ascontiguousarray(p[b])) for b in range(x.shape[0])]


def kernel(**inputs):
    in_maps = make_in_maps(inputs)
    nc = build()
    res = run_bass_kernel_spmd(nc, in_maps, core_ids=list(range(8)))
    return np.stack([r["out"] for r in res.results], axis=0).astype(np.float32)
```

```python
import math
import os
import numpy as np
from contextlib import ExitStack
import ml_dtypes
import concourse.bass as bass
import concourse.mybir as mybir
from concourse.bass_utils import run_bass_kernel_spmd

F32 = mybir.dt.float32
BF16 = mybir.dt.bfloat16
ALU = mybir.AluOpType
AF = mybir.ActivationFunctionType

ENGS = ["pe", "act", "dve", "pool", "sp"]

S = 4096
D = 1024
NT = 32
NG = 8
SC_MLA = 96 ** -0.5
SC_SWA = 64 ** -0.5
EPS = 1e-6


class Prog:
    def __init__(self, nc, stack):
        self.nc = nc
        self.stack = stack
        self.q = {e: [] for e in ENGS}
        self.cnt = {}
        self.seen = {e: {} for e in ENGS}
        self.sems = {}
        self.lastw = {}
        self.readers = {}
        for e in ENGS:
            self._sem(e)

    def _sem(self, name):
        if name not in self.sems:
            self.sems[name] = self.stack.enter_context(self.nc.semaphore("s_" + name))
            self.cnt[name] = 0
        return self.sems[name]

    def _wait(self, eng, s, v):
        if eng == "pe" and s == "pe":
            return
        if self.seen[eng].get(s, 0) < v:
            self.seen[eng][s] = v
            sem = self.sems[s]
            self.q[eng].append(lambda h, sem=sem, v=v: h.wait_ge(sem, v))

    def _deps(self, eng, reads, writes):
        need = {}

        def add(t):
            if t is None:
                return
            s, v = t
            if need.get(s, 0) < v:
                need[s] = v

        for r in reads:
            add(self.lastw.get(r))
        for w in writes:
            add(self.lastw.get(w))
            for t in self.readers.get(w, ()):
                add(t)
        for s, v in need.items():
            self._wait(eng, s, v)

    def _commit(self, ticket, reads, writes):
        for r in reads:
            lst = self.readers.setdefault(r, [])
            lst[:] = [t for t in lst if t[0] != ticket[0]]
            lst.append(ticket)
        for w in writes:
            self.lastw[w] = ticket
            self.readers[w] = []

    def op(self, eng, fn, reads=(), writes=()):
        reads = list(reads)
        writes = list(writes)
        self._deps(eng, reads, writes)
        self.cnt[eng] += 1
        sem = self.sems[eng]
        self.q[eng].append(lambda h, sem=sem, fn=fn: fn(h).then_inc(sem, 1))
        t = (eng, self.cnt[eng])
        self._commit(t, reads, writes)
        return t

    def dma(self, eng, semname, fn, reads=(), writes=()):
        reads = list(reads)
        writes = list(writes)
        sem = self._sem(semname)
        self._deps(eng, reads, writes)
        self.cnt[semname] += 16
        self.q[eng].append(lambda h, sem=sem, fn=fn: fn(h).then_inc(sem, 16))
        t = (semname, self.cnt[semname])
        self._commit(t, reads, writes)
        return t

    def wait_all(self, eng, keys):
        self._deps(eng, list(keys), [])

    def barrier(self, final=False):
        keep = self.lastw.get("scratch")
        for e in ENGS:
            for s, c in self.cnt.items():
                if c > 0 and (final or s != "d_scr"):
                    self._wait(e, s, c)
        self.lastw.clear()
        self.readers.clear()
        if keep is not None and not final:
            self.lastw["scratch"] = keep

    def emit(self):
        nc = self.nc
        q = self.q
        with nc.Block() as block:
            @block.tensor
            def _(h):
                for f in q["pe"]:
                    f(h)

            @block.scalar
            def _(h):
                for f in q["act"]:
                    f(h)

            @block.vector
            def _(h):
                for f in q["dve"]:
                    f(h)

            @block.gpsimd
            def _(h):
                for f in q["pool"]:
                    f(h)

            @block.sync
            def _(h):
                for f in q["sp"]:
                    f(h)


def build(stage=3, dbg=()):
    nc = bass.Bass("TRN2", target_bir_lowering=False)

    def din(name, shape, dt=F32):
        return nc.dram_tensor(name, list(shape), dt, kind="ExternalInput").ap()

    x = din("x", [S, D])
    p_in = din("p", [S, 256])
    w_in = din("w_in", [1024, 3744])
    w_q_b = din("w_q_b", [256, 1536])
    w_kv_b = din("w_kv_b", [128, 2048])
    w_mla_up = din("w_mla_up", [1024, 1024])
    w_swa_up = din("w_swa_up", [1024, 1024])
    w_out = din("w_out", [1024, 1024])
    w_mlp_up = din("w_mlp_up", [1024, 4096])
    w_mlp_down = din("w_mlp_down", [4096, 1024])
    w_ple = din("w_ple", [256, 1024])
    w_ple_gate = din("w_ple_gate", [1024, 1024])
    gcol = din("gcol", [128, 19])
    grow = din("grow", [128, 3, 1024])
    sinks_b = din("sinks_b", [128, 16])
    ident = din("ident", [128, 128], BF16)
    maskD = din("maskD", [128, 128], BF16)
    maskP = din("maskP", [128, 128], BF16)
    ropeK = din("ropeK", [128, 32, 64])
    ropeS = din("ropeS", [128, 32, 64])
    ropeQ = din("ropeQ", [2, 128, S])
    out = nc.dram_tensor("out", [S, D], F32, kind="ExternalOutput").ap()
    dbg_t = {}
    for name, shape in dbg:
        dbg_t[name] = nc.dram_tensor("dbg_" + name, list(shape), F32, kind="ExternalOutput").ap()

    sc_in = nc.dram_tensor("sc_in", [1024, 3744], BF16).ap()
    sc_upA = nc.dram_tensor("sc_upA", [1024, 1024], BF16).ap()
    sc_upB = nc.dram_tensor("sc_upB", [1024, 1024], BF16).ap()
    sc_out = nc.dram_tensor("sc_out", [1024, 1024], BF16).ap()
    sc_w1 = nc.dram_tensor("sc_w1", [1024, 4096], BF16).ap()
    sc_w2 = nc.dram_tensor("sc_w2", [4096, 1024], BF16).ap()
    sc_pg = nc.dram_tensor("sc_pg", [1024, 1024], BF16).ap()

    def sbuf(st, n, s, d):
        return st.enter_context(nc.sbuf_tensor(n, list(s), d))

    def psum(st, n, s, d):
        return st.enter_context(nc.psum_tensor(n, list(s), d))

    with ExitStack() as st0:
        P = Prog(nc, st0)

        def mm(out_, lhsT, rhs, start=True, stop=True, reads=(), writes=()):
            return P.op("pe", lambda h: h.matmul(out_, lhsT=lhsT, rhs=rhs, start=start, stop=stop), reads, writes)

        def tr(out_, in_, idn, reads=(), writes=()):
            return P.op("pe", lambda h: h.transpose(out=out_, in_=in_, identity=idn), reads, writes)

        def act(out_, in_, func, reads=(), writes=(), eng="act", **kw):
            return P.op(eng, lambda h: h.activation(out=out_, in_=in_, func=func, **kw), reads, writes)

        def tcopy(eng, out_, in_, reads=(), writes=()):
            if eng == "act":
                return P.op(eng, lambda h: h.copy(out=out_, in_=in_), reads, writes)
            return P.op(eng, lambda h: h.tensor_copy(out=out_, in_=in_), reads, writes)

        def tt(eng, out_, in0, in1, op, reads=(), writes=()):
            return P.op(eng, lambda h: h.tensor_tensor(out=out_, in0=in0, in1=in1, op=op), reads, writes)

        def ts(eng, out_, in0, s1, s2, op0, op1=None, reads=(), writes=(), **kw):
            if op1 is None:
                return P.op(eng, lambda h: h.tensor_scalar(out=out_, in0=in0, scalar1=s1, scalar2=None, op0=op0, **kw), reads, writes)
            return P.op(eng, lambda h: h.tensor_scalar(out=out_, in0=in0, scalar1=s1, scalar2=s2, op0=op0, op1=op1, **kw), reads, writes)

        def stt(eng, out_, in0, scalar, in1, op0, op1, reads=(), writes=()):
            return P.op(eng, lambda h: h.scalar_tensor_tensor(out=out_, in0=in0, scalar=scalar, in1=in1, op0=op0, op1=op1), reads, writes)

        def memset(eng, ap, val, writes=()):
            return P.op(eng, lambda h: h.memset(ap, val), (), writes)

        uniq = [0]

        def dma(eng, sem, out_, in_, reads=(), writes=()):
            if sem in ("d_c", "d_w2"):
                uniq[0] += 1
                sem = f"d_u{uniq[0]}"
            return P.dma(eng, sem, lambda h: h.dma_start(out=out_, in_=in_), reads, writes)

        def dump(name, src_ap, key):
            if name in dbg_t:
                dma("sp", "d_dbg", dbg_t[name], src_ap, reads=[key], writes=["dbg_" + name])

        o_mlaT = sbuf(st0, "o_mlaT", [128, 8, S], BF16)
        identb = sbuf(st0, "identb", [128, 128], BF16)
        maskD_sb = sbuf(st0, "maskD_sb", [128, 128], BF16)
        maskP_sb = sbuf(st0, "maskP_sb", [128, 128], BF16)
        gcol_sb = sbuf(st0, "gcol_sb", [128, 19], F32)
        epsb = sbuf(st0, "epsb", [128, 1], F32)
        ones32 = sbuf(st0, "ones32", [128, 128], F32)

        dma("sp", "d_c", identb[:], ident, writes=["identb"])
        dma("sp", "d_c", maskD_sb[:], maskD, writes=["maskD"])
        dma("sp", "d_c", maskP_sb[:], maskP, writes=["maskP"])
        dma("sp", "d_c", gcol_sb[:], gcol, writes=["gcol"])
        memset("pool", epsb[:], EPS, writes=["epsb"])
        memset("pool", ones32[:], 1.0, writes=["ones32"])

        def rstd_from_ss(ss_ap, out_ap, lnv_ap, inv_n, rkey, wkey, lkey):
            act(lnv_ap, ss_ap, AF.Ln, reads=[rkey, "epsb"], writes=[lkey], scale=inv_n, bias=epsb[:, 0:1])
            act(out_ap, lnv_ap, AF.Exp, reads=[lkey], writes=[wkey], scale=-0.5)

        cast_jobs = []

        def cast_rows(dst, src, rows):
            for r0 in range(0, rows, 128):
                cast_jobs.append((dst[r0:r0 + 128, :], src[r0:r0 + 128, :]))

        def cast_emit(n=1):
            for _ in range(n):
                if cast_jobs:
                    d_, s_ = cast_jobs.pop(0)
                    dma("pool", "d_scr", d_, s_, writes=["scratch"])

        with ExitStack() as st12:
            qaT = sbuf(st12, "qaT", [128, 2, S], BF16)
            kvaT = sbuf(st12, "kvaT", [128, S], BF16)
            kT = [sbuf(st12, f"kT{b}", [96, S], BF16) for b in range(2)]
            wqx = sbuf(st12, "wqx", [128, 2, 16, 128], BF16)
            wkvb = sbuf(st12, "wkvb", [128, 2048], BF16)

            with ExitStack() as st1:
                NB1 = 6
                w_g0 = sbuf(st1, "w_g0", [128, 8, 448], BF16)
                ropeK_sb = sbuf(st1, "ropeK_sb", [128, 32, 64], F32)
                xt = [sbuf(st1, f"p1xt{i}", [128, D], F32) for i in range(NB1)]
                junk = [sbuf(st1, f"p1junk{i}", [128, D], BF16) for i in range(NB1)]
                st_sb = [sbuf(st1, f"p1st_sb{i}", [128, 8], F32) for i in range(NB1)]
                xn = [sbuf(st1, f"p1xn{i}", [128, D], BF16) for i in range(NB1)]
                hT = [sbuf(st1, f"p1hT{i}", [128, 8, 128], BF16) for i in range(NB1)]
                qan = [sbuf(st1, f"p1qan{i}", [128, 256], BF16) for i in range(NB1)]
                kvan = [sbuf(st1, f"p1kvan{i}", [128, 128], BF16) for i in range(NB1)]
                krm = [sbuf(st1, f"p1krm{i}", [128, 128], BF16) for i in range(NB1)]
                kt12 = [sbuf(st1, f"p1kt12{i}", [128, 64], F32) for i in range(NB1)]
                pT = [psum(st1, f"pT{i}", [128, 8, 128], BF16) for i in range(2)]
                pG0 = [psum(st1, f"pG0{i}", [128, 512], F32) for i in range(3)]
                pTB = [psum(st1, f"pTB{i}", [128, 8, 128], BF16) for i in range(2)]

                dma("pool", "d_w1", w_g0[:, :, 0:416], w_in[:, 0:416].rearrange("(kc p) c -> p kc c", p=128), writes=["w_g0"])
                dma("sp", "d_c", ropeK_sb[:], ropeK, writes=["ropeK"])
                ts("pool", w_g0[:, :, 416:432], w_g0[:, :, 400:416], -1.0, None, ALU.mult, reads=["w_g0"], writes=["w_g0r"])
                tcopy("pool", w_g0[:, :, 432:448], w_g0[:, :, 384:400], reads=["w_g0"], writes=["w_g0r2"])
                for i in range(NB1):
                    memset("pool", krm[i][:], 0.0, writes=[f"krm{i}"])
                for kc_ in range(2):
                    dma("pool", "d_w2", wqx[:, kc_, :, 0:96], w_q_b[kc_ * 128:(kc_ + 1) * 128, :].rearrange("p (h c) -> p h c", c=96), writes=[f"wqb{kc_}"])
                dma("pool", "d_w2", wkvb[:], w_kv_b, writes=["wkvb"])
                ts("pool", wqx[:, :, :, 96:112], wqx[:, :, :, 80:96], -1.0, None, ALU.mult, reads=["wqb0", "wqb1"], writes=["wqrot"])
                tcopy("pool", wqx[:, :, :, 112:128], wqx[:, :, :, 64:80], reads=["wqb0", "wqb1"], writes=["wqrot2"])

                if stage >= 3:
                    cast_rows(sc_in, w_in, 1024)
                    cast_rows(sc_upA, w_mla_up, 1024)
                    cast_rows(sc_upB, w_swa_up, 1024)
                    cast_rows(sc_out, w_out, 1024)
                    cast_rows(sc_w1, w_mlp_up, 1024)
                    cast_rows(sc_w2, w_mlp_down, 4096)
                    cast_rows(sc_pg, w_ple_gate, 1024)

                def p1_stage(t, stage_):
                    i = t % NB1
                    K_ = lambda n: f"{n}{i}"
                    xb, jk, sb_, xnb, hTb = xt[i], junk[i], st_sb[i], xn[i], hT[i]
                    pTt, pTk = pT[t % 2], f"pT{t % 2}"
                    pG, pGk = pG0[t % 3], f"pG0{t % 3}"
                    pTBt, pTBk = pTB[t % 2], f"pTB{t % 2}"
                    if stage_ == 0:
                        dma("sp", f"d_x{i}", xb[:], x[t * 128:(t + 1) * 128, :], writes=[K_("xt")])
                        act(jk[:], xb[:], AF.Square, reads=[K_("xt")], writes=[K_("junk"), K_("ss")], accum_out=sb_[:, 0:1])
                        rstd_from_ss(sb_[:, 0:1], sb_[:, 2:3], sb_[:, 1:2], 1.0 / D, K_("ss"), K_("rstd"), K_("lnv"))
                        ts("dve", xnb[:], xb[:], sb_[:, 2:3], None, ALU.mult, reads=[K_("xt"), K_("rstd")], writes=[K_("xn")])
                    elif stage_ == 1:
                        for kc in range(8):
                            tr(pTt[:, kc, :], xnb[:, kc * 128:(kc + 1) * 128], identb[:], reads=[K_("xn"), "identb"], writes=[pTk])
                        tt("dve", hTb[:], pTt[:], gcol_sb[:, 0:8].unsqueeze(2).broadcast_to([128, 8, 128]), ALU.mult,
                           reads=[pTk, "gcol"], writes=[K_("hT")])
                    elif stage_ == 2:
                        for kc in range(8):
                            mm(pG[:, 0:448], hTb[:, kc, :], w_g0[:, kc, :], start=(kc == 0), stop=(kc == 7),
                               reads=[K_("hT"), "w_g0", "w_g0r", "w_g0r2"], writes=[pGk])
                        act(jk[:, 0:256], pG[:, 0:256], AF.Square, reads=[pGk], writes=[K_("junk"), K_("ss2a")], accum_out=sb_[:, 3:4])
                        act(jk[:, 256:384], pG[:, 256:384], AF.Square, reads=[pGk], writes=[K_("junk"), K_("ss2b")], accum_out=sb_[:, 4:5])
                        rstd_from_ss(sb_[:, 3:4], sb_[:, 5:6], sb_[:, 6:7], 1.0 / 256, K_("ss2a"), K_("rs2a"), K_("lnv2a"))
                        rstd_from_ss(sb_[:, 4:5], sb_[:, 7:8], sb_[:, 6:7], 1.0 / 128, K_("ss2b"), K_("rs2b"), K_("lnv2a"))
                    elif stage_ == 3:
                        ts("dve", qan[i][:], pG[:, 0:256], sb_[:, 5:6], None, ALU.mult, reads=[pGk, K_("rs2a")], writes=[K_("qan")])
                        ts("dve", kvan[i][:], pG[:, 256:384], sb_[:, 7:8], None, ALU.mult, reads=[pGk, K_("rs2b")], writes=[K_("kvan")])
                        tt("dve", kt12[i][:], pG[:, 384:448], ropeK_sb[:, t, :], ALU.mult, reads=[pGk, "ropeK"], writes=[K_("kt12")])
                        tt("dve", krm[i][:, 64:96], kt12[i][:, 0:32], kt12[i][:, 32:64], ALU.add, reads=[K_("kt12")], writes=[K_("krm")])
                    else:
                        tr(pTBt[:, 0, :], qan[i][:, 0:128], identb[:], reads=[K_("qan"), "identb"], writes=[pTBk])
                        tr(pTBt[:, 1, :], qan[i][:, 128:256], identb[:], reads=[K_("qan"), "identb"], writes=[pTBk])
                        tr(pTBt[:, 2, :], kvan[i][:], identb[:], reads=[K_("kvan"), "identb"], writes=[pTBk])
                        tr(pTBt[:, 3, :], krm[i][:], identb[:], reads=[K_("krm"), "identb"], writes=[pTBk])
                        tsl = slice(t * 128, (t + 1) * 128)
                        tt("dve", qaT[:, :, tsl], pTBt[:, 0:2, :], gcol_sb[:, 16:18].unsqueeze(2).broadcast_to([128, 2, 128]), ALU.mult,
                           reads=[pTBk, "gcol"], writes=["qaT"])
                        ts("dve", kvaT[:, tsl], pTBt[:, 2, :], gcol_sb[:, 18:19], None, ALU.mult, reads=[pTBk, "gcol"], writes=["kvaT"])
                        tcopy("dve", kT[0][64:96, tsl], pTBt[64:96, 3, :], reads=[pTBk], writes=["kT0r"])
                        tcopy("dve", kT[1][64:96, tsl], pTBt[64:96, 3, :], reads=[pTBk], writes=["kT1r"])

                NST = 5
                for it in range(NT + NST - 1):
                    for k in range(NST):
                        if 0 <= it - k < NT:
                            p1_stage(it - k, k)
                P.barrier()
                if "qaT" in dbg_t:
                    for kc in range(2):
                        for c in range(4):
                            tcopy("dve", xt[0][:], qaT[:, kc, c * 1024:(c + 1) * 1024], reads=["qaT"], writes=["xt0"])
                            dma("sp", "d_dbg", dbg_t["qaT"][kc, :, c * 1024:(c + 1) * 1024], xt[0][:], reads=["xt0"], writes=["dbg"])
                    for c in range(4):
                        tcopy("dve", xt[0][0:96, :], kT[0][:, c * 1024:(c + 1) * 1024], reads=["qaT"], writes=["xt0"])
                        dma("sp", "d_dbg", dbg_t["kT"][:, c * 1024:(c + 1) * 1024], xt[0][0:96, :], reads=["xt0"], writes=["dbg"])
                    P.barrier()

            if stage >= 2:
                with ExitStack() as st2:
                    cosT = sbuf(st2, "cosT", [128, S], F32)
                    sinT = sbuf(st2, "sinT", [128, S], F32)
                    Vp = [sbuf(st2, f"Vp{b}", [128, 32, 128], BF16) for b in range(2)]
                    qT = [sbuf(st2, f"qT{b}", [96, S], BF16) for b in range(2)]
                    NPT = 6
                    PT = [sbuf(st2, f"PT{i}", [128, 512], BF16) for i in range(NPT)]
                    rt1 = sbuf(st2, "rt1", [128, 512], F32)
                    rt2 = sbuf(st2, "rt2", [128, 512], F32)
                    recrow = [sbuf(st2, f"recrow{i}", [128, 512], F32) for i in range(2)]

                    NPS = 3
                    pS = [psum(st2, f"pS{i}", [128, 512], F32) for i in range(NPS)]
                    NPO = 3
                    pO = [psum(st2, f"pO{i}", [128, 512], F32) for i in range(NPO)]
                    pgen = [psum(st2, f"pgen{i}", [128, 512], F32) for i in range(2)]

                    dma("sp", "d_c", cosT[:], ropeQ[0], writes=["cosT"])
                    dma("sp", "d_c", sinT[:], ropeQ[1], writes=["sinT"])
                    for b in range(2):
                        memset("pool", Vp[b][:], 1.0, writes=[f"Vp{b}"])

                    gen_i = [0]

                    def gen_steps(h):
                        b = h % 2
                        steps = []

                        def k_step(c):
                            pg = pgen[gen_i[0] % 2]
                            pk = f"pgen{gen_i[0] % 2}"
                            gen_i[0] += 1
                            mm(pg[0:64, :], wkvb[:, h * 128:h * 128 + 64], kvaT[:, c * 512:(c + 1) * 512],
                               reads=["wkvb"], writes=[pk])
                            tcopy("dve", kT[b][0:64, c * 512:(c + 1) * 512], pg[0:64, :], reads=[pk], writes=[f"kT{b}"])

                        def v_step(tg):
                            pg = pgen[gen_i[0] % 2]
                            pk = f"pgen{gen_i[0] % 2}"
                            gen_i[0] += 1
                            for i in range(8):
                                tl = tg * 8 + i
                                mm(pg[:, i * 64:(i + 1) * 64], kvaT[:, tl * 128:(tl + 1) * 128], wkvb[:, h * 128 + 64:h * 128 + 128],
                                   reads=["wkvb"], writes=[pk])
                            tcopy("dve", Vp[b][:, tg * 8:(tg + 1) * 8, 0:64], pg[:].rearrange("p (i c) -> p i c", c=64),
                                  reads=[pk], writes=[f"Vp{b}"])

                        def q_step(c):
                            cs = slice(c * 512, (c + 1) * 512)
                            pq = pgen[gen_i[0] % 2]
                            pqk = f"pgen{gen_i[0] % 2}"
                            gen_i[0] += 1
                            for kc in range(2):
                                mm(pq[:, :], wqx[:, kc, h, :], qaT[:, kc, cs], start=(kc == 0), stop=(kc == 1),
                                   reads=["wqb0", "wqb1", "wqrot", "wqrot2"], writes=[pqk])
                            tcopy("dve", qT[b][0:64, cs], pq[0:64, :], reads=[pqk], writes=[f"qT{b}"])
                            tt("dve", rt1[64:96, :], pq[64:96, :], cosT[64:96, cs], ALU.mult, reads=[pqk, "cosT"], writes=["rt1"])
                            tt("dve", rt2[64:96, :], pq[96:128, :], sinT[96:128, cs], ALU.mult, reads=[pqk, "sinT"], writes=["rt2"])
                            tt("pool", qT[b][64:96, cs], rt1[64:96, :], rt2[64:96, :], ALU.add, reads=["rt1", "rt2"], writes=[f"qT{b}"])

                        for c in range(8):
                            steps.append(lambda c=c: k_step(c))
                            if c % 2 == 0:
                                steps.append(lambda tg=c // 2: v_step(tg))
                            steps.append(lambda c=c: q_step(c))
                        return steps

                    def gen_head(h):
                        for f in gen_steps(h):
                            f()

                    units = []
                    for h in range(16):
                        for c in range(8):
                            nkb = 4 * c + 4
                            for kb in range(nkb):
                                q0 = (kb - 4 * c) * 128 if kb >= 4 * c else 0
                                units.append((h, c, kb, q0, kb == 0, kb == nkb - 1, nkb))
                    LOOK = 2
                    pending = []
                    infos = {}

                    def qk(j):
                        h, c, kb, q0, first, last, nkb = units[j]
                        b = h % 2
                        ps_, psk = pS[j % NPS], f"pS{j % NPS}"
                        pt_, ptk = PT[j % NPT], f"PT{j % NPT}"
                        diag = kb >= 4 * c
                        kblk = kT[b][0:96, kb * 128:(kb + 1) * 128]
                        rk = [f"kT{b}", f"kT{b}r", f"qT{b}"]
                        mm(ps_[:, q0:512], kblk, qT[b][0:96, c * 512 + q0:(c + 1) * 512],
                           start=True, stop=(not diag), reads=rk, writes=[psk])
                        if diag:
                            mm(ps_[:, q0:q0 + 128], identb[:], maskD_sb[:], start=False, stop=True,
                               reads=["identb", "maskD"], writes=[psk])
                        act(pt_[:, q0:512], ps_[:, q0:512], AF.Exp, reads=[psk], writes=[ptk], scale=SC_MLA)
                        infos[j] = (pt_, ptk)

                    def pv(j):
                        h, c, kb, q0, first, last, nkb = units[j]
                        b = h % 2
                        ci = h * 8 + c
                        po, pok = pO[ci % NPO], f"pO{ci % NPO}"
                        pt_, ptk = infos.pop(j)
                        mm(po[:, q0:512], Vp[b][:, kb, :], pt_[:, q0:512], start=first, stop=last,
                           reads=[f"Vp{b}", ptk], writes=[pok])
                        if last:
                            rr, rrk = recrow[ci % 2], f"recrow{ci % 2}"
                            P.op("dve", lambda hh, rr=rr, po=po: hh.reciprocal(out=rr[64:128, :], in_=po[64:128, :]), reads=[pok], writes=[rrk])
                            hp, hs = h // 2, (h % 2) * 64
                            tt("dve", o_mlaT[hs:hs + 64, hp, c * 512:(c + 1) * 512], po[0:64, :], rr[64:128, :], ALU.mult,
                               reads=[pok, rrk], writes=["o_mlaT"])
                        return None

                    gen_head(0)
                    gsteps = []
                    NU = len(units)
                    for j in range(NU + LOOK):
                        if j < NU:
                            h, c, kb = units[j][0], units[j][1], units[j][2]
                            if c == 0 and kb == 0:
                                while gsteps:
                                    gsteps.pop(0)()
                                if h + 1 < 16:
                                    gsteps.extend(gen_steps(h + 1))
                            elif gsteps and j % 6 == 0:
                                gsteps.pop(0)()
                            if j % 20 == 10:
                                cast_emit(1)
                            qk(j)
                        if j >= LOOK:
                            f = pv(j - LOOK)
                            if f is not None:
                                jn = j - LOOK + 1
                                n_next = units[jn][6] if jn < NU else 0
                                pending.append((j + max(0, min(6, n_next - 1)), f))
                        while pending and pending[0][0] <= j:
                            pending.pop(0)[1]()
                    while pending:
                        pending.pop(0)[1]()
                    cast_emit(len(cast_jobs))
                    P.barrier()
                    if "o_mlaT" in dbg_t:
                        for j in range(8):
                            for c in range(8):
                                tcopy("dve", rt1[:], o_mlaT[:, j, c * 512:(c + 1) * 512], reads=["o_mlaT"], writes=["rt1"])
                                dma("sp", "d_dbg", dbg_t["o_mlaT"][j, :, c * 512:(c + 1) * 512], rt1[:], reads=["rt1"], writes=["dbg"])
                        P.barrier()
        P.barrier()
        if stage >= 3:
            with ExitStack() as st3:
                x_g = sbuf(st3, "x_g", [128, 4, D], F32)
                hT3 = sbuf(st3, "hT3", [128, 8, 512], BF16)
                bufA = sbuf(st3, "bufA", [128, 8192], BF16)
                swa_qT = bufA[:, 0:4096].rearrange("p (t j q) -> p t j q", j=8, q=128)
                o_swaT = bufA[:, 4096:8192].rearrange("p (j t) -> p j t", t=512)
                x2acc = bufA[:].bitcast(F32).rearrange("p (t c) -> p t c", c=1024)
                yT = sbuf(st3, "yT", [128, 8, 512], BF16)
                uT = sbuf(st3, "uT", [128, 8, 512], BF16)
                NS = 3
                ring = [sbuf(st3, f"ring{i}", [128, 8, 512], BF16) for i in range(NS)]
                grow_sb = sbuf(st3, "grow_sb", [128, 3, 1024], F32)
                wpp = sbuf(st3, "wpp", [128, 2, 1024], BF16)
                swa_kT = sbuf(st3, "swa_kT", [128, 8 * 128], BF16)
                Vs = sbuf(st3, "Vs", [128, 8, 2, 65], BF16)
                xn3s = [sbuf(st3, f"xn3_{i}", [128, D], BF16) for i in range(2)]
                junk3 = sbuf(st3, "junk3", [128, D], BF16)
                ropeS_g = sbuf(st3, "ropeS_g", [128, 4, 64], F32)
                qr_tms = [sbuf(st3, f"qr_tm{i}", [128, 8, 2, 64], BF16) for i in range(2)]
                kr_tms = [sbuf(st3, f"kr_tm{i}", [128, 2, 2, 32], BF16) for i in range(2)]
                PTs = [sbuf(st3, f"PTs{i}", [128, 1024], BF16) for i in range(4)]
                esink = sbuf(st3, "esink", [128, 16], F32)
                dn = sbuf(st3, "dn", [128, 16], F32)
                o_tm = sbuf(st3, "o_tm", [128, 16, 64], BF16)
                one_b = sbuf(st3, "one_b", [128, 1], F32)
                mt = [sbuf(st3, f"mt{i}", [128, 512], F32) for i in range(2)]
                stg = [sbuf(st3, f"stg{i}", [128, D], F32) for i in range(2)]
                stg2 = sbuf(st3, "stg2", [128, D], F32)
                rtmp = [stg2[:, i * 256:(i + 1) * 256].rearrange("p (h c) -> p h c", c=32) for i in range(4)]
                pt_sbs = [sbuf(st3, f"pt_sb{i}", [128, 256], F32) for i in range(2)]
                pn_sb = sbuf(st3, "pn_sb", [128, 256], BF16)
                pT_sb = sbuf(st3, "pT_sb", [128, 2, 512], BF16)
                ss3 = sbuf(st3, "ss3", [128, 8], F32)
                ln3 = sbuf(st3, "ln3", [128, 8], F32)
                rs3 = sbuf(st3, "rs3", [128, 8], F32)
                pT3 = psum(st3, "pT3", [128, 8, 128], BF16)
                pX = psum(st3, "pX", [128, 512], F32)
                pD = [psum(st3, f"pD{i}", [128, 1024], F32) for i in range(3)]

                dma("sp", "d_c", grow_sb[:], grow, writes=["grow"])
                dma("pool", "d_w3", wpp[:], w_ple.rearrange("(kc p) c -> p kc c", p=128), writes=["wpp"])
                dma("sp", "d_c", esink[:], sinks_b, writes=["esink0"])
                act(esink[:], esink[:], AF.Exp, reads=["esink0"], writes=["esink"])
                memset("pool", one_b[:], 1.0, writes=["one_b"])
                memset("pool", Vs[:], 1.0, writes=["Vs"])

                def wsrc(sc, r0, c0, ncols):
                    return (sc[r0:r0 + 1024, c0:c0 + ncols].rearrange("(kc p) c -> p kc c", p=128), ncols)

                per_group = [wsrc(sc_in, 0, 416, 512), wsrc(sc_in, 0, 928, 512), wsrc(sc_in, 0, 1440, 256)]
                for half in range(2):
                    per_group += [wsrc(sc_in, 0, 1696 + half * 512, 512), wsrc(sc_in, 0, 2720 + half * 512, 512),
                                  wsrc(sc_upA, 0, half * 512, 512), wsrc(sc_upB, 0, half * 512, 512)]
                per_group += [wsrc(sc_out, 0, 0, 512), wsrc(sc_out, 0, 512, 512)]
                for s_ in range(4):
                    per_group += [wsrc(sc_w1, 0, s_ * 1024, 512), wsrc(sc_w1, 0, s_ * 1024 + 512, 512),
                                  wsrc(sc_w2, s_ * 1024, 0, 512), wsrc(sc_w2, s_ * 1024, 512, 512)]
                per_group += [wsrc(sc_pg, 0, 0, 512), wsrc(sc_pg, 0, 512, 512)]
                srcs = per_group * NG
                rs = {"load": 0, "use": 0}

                def ring_load():
                    i = rs["load"]
                    if i >= len(srcs):
                        return
                    slot = i % NS
                    src, ncols = srcs[i]
                    dma("sp", f"d_ring{slot}", ring[slot][:, :, 0:ncols], src, reads=["scratch"], writes=[f"ring{slot}"])
                    rs["load"] += 1

                def ring_get():
                    i = rs["use"]
                    assert i < rs["load"]
                    rs["use"] += 1
                    return ring[i % NS], f"ring{i % NS}"

                def ring_release(n=1):
                    for _ in range(n):
                        ring_load()

                for _ in range(NS):
                    ring_load()

                pd_i = [0]

                def next_pd():
                    i = pd_i[0] % 3
                    pd_i[0] += 1
                    return pD[i], f"pD{i}"

                st_i = [0]

                def rms_tile(src_ap, src_key, col):
                    act(junk3[:], src_ap, AF.Square, reads=[src_key], writes=["junk3", f"ss3_{col}"], accum_out=ss3[:, col:col + 1])
                    rstd_from_ss(ss3[:, col:col + 1], rs3[:, col:col + 1], ln3[:, col:col + 1], 1.0 / D, f"ss3_{col}", f"rs3_{col}", f"ln3_{col}")

                xn_i = [0]

                def to_hT(t, gcols, src_ap, src_key, col):
                    xn3 = xn3s[xn_i[0] % 2]
                    xk = f"xn3_{xn_i[0] % 2}"
                    xn_i[0] += 1
                    if col is not None:
                        ts("dve", xn3[:], src_ap, rs3[:, col:col + 1], None, ALU.mult, reads=[src_key, f"rs3_{col}"], writes=[xk])
                    else:
                        tcopy("dve", xn3[:], src_ap, reads=[src_key], writes=[xk])
                    for kc in range(8):
                        tr(pT3[:, kc, :], xn3[:, kc * 128:(kc + 1) * 128], identb[:], reads=[xk, "identb"], writes=["pT3"])
                    if gcols is not None:
                        tt("dve", hT3[:, :, t * 128:(t + 1) * 128], pT3[:], gcol_sb[:, gcols:gcols + 8].unsqueeze(2).broadcast_to([128, 8, 128]),
                           ALU.mult, reads=["pT3", "gcol"], writes=[f"hT3_{t}"])
                    else:
                        tcopy("dve", hT3[:, :, t * 128:(t + 1) * 128], pT3[:], reads=["pT3"], writes=[f"hT3_{t}"])

                def sigmoid_from(ps_ap, ps_key, tmp_ap, tmp_key, out_ap, out_key):
                    act(tmp_ap, ps_ap, AF.Exp, reads=[ps_key], writes=[tmp_key], scale=-1.0)
                    act(tmp_ap, tmp_ap, AF.Ln, reads=[tmp_key, "one_b"], writes=[tmp_key], bias=one_b[:, 0:1], scale=1.0)
                    act(out_ap, tmp_ap, AF.Exp, reads=[tmp_key], writes=[out_key], scale=-1.0)

                def s1_tile(g_, t):
                    T = 4 * g_ + t
                    dma("sp", f"d_xg{t}", x_g[:, t, :], x[T * 128:(T + 1) * 128, :], writes=[f"x_g{t}"])
                    rms_tile(x_g[:, t, :], f"x_g{t}", t)
                    to_hT(t, 0, x_g[:, t, :], f"x_g{t}", t)

                for g in range(NG):
                    gs = slice(g * 512, (g + 1) * 512)
                    dma("sp", "d_rs", ropeS_g[:], ropeS[:, 4 * g:4 * g + 4, :], writes=["ropeS_g"])
                    if g == 0:
                        for t in range(4):
                            s1_tile(0, t)
                    wlo, wlok = ring_get()
                    whi, whik = ring_get()
                    wkv, wkvk = ring_get()
                    def s2_A(t):
                        T = 4 * g + t
                        qr_tm, qrk = qr_tms[t % 2], f"qr_tm{t % 2}"
                        kr_tm, krk = kr_tms[t % 2], f"kr_tm{t % 2}"
                        tsl = slice(t * 128, (t + 1) * 128)
                        cosb = ropeS_g[:, t, 0:32]
                        sinb = ropeS_g[:, t, 32:64]
                        for hi_, (wr, wrk) in enumerate(((wlo, wlok), (whi, whik))):
                            pd, pdk = next_pd()
                            for kc in range(8):
                                mm(pd[:, 0:512], hT3[:, kc, tsl], wr[:, kc, 0:512], start=(kc == 0), stop=(kc == 7),
                                   reads=[f"hT3_{t}", wrk], writes=[pdk])
                            q4 = pd[:, 0:512].rearrange("p (h two c) -> p h two c", two=2, c=32)
                            cb = cosb.unsqueeze(1).broadcast_to([128, 8, 32])
                            sb_ = sinb.unsqueeze(1).broadcast_to([128, 8, 32])
                            tt("dve", rtmp[0], q4[:, :, 0, :], cb, ALU.mult, reads=[pdk, "ropeS_g"], writes=["rtmp0"])
                            tt("dve", rtmp[1], q4[:, :, 1, :], sb_, ALU.mult, reads=[pdk, "ropeS_g"], writes=["rtmp1"])
                            tt("dve", rtmp[2], q4[:, :, 1, :], cb, ALU.mult, reads=[pdk, "ropeS_g"], writes=["rtmp2"])
                            tt("dve", rtmp[3], q4[:, :, 0, :], sb_, ALU.mult, reads=[pdk, "ropeS_g"], writes=["rtmp3"])
                            tt("pool", qr_tm[:, :, hi_, 0:32], rtmp[0], rtmp[1], ALU.subtract, reads=["rtmp0", "rtmp1"], writes=[qrk])
                            tt("pool", qr_tm[:, :, hi_, 32:64], rtmp[2], rtmp[3], ALU.add, reads=["rtmp2", "rtmp3"], writes=[qrk])
                        pd, pdk = next_pd()
                        for kc in range(8):
                            mm(pd[:, 0:256], hT3[:, kc, tsl], wkv[:, kc, 0:256], start=(kc == 0), stop=(kc == 7),
                               reads=[f"hT3_{t}", wkvk], writes=[pdk])
                        k4 = pd[:, 0:128].rearrange("p (h two c) -> p h two c", two=2, c=32)
                        cb2 = cosb.unsqueeze(1).broadcast_to([128, 2, 32])
                        sb2 = sinb.unsqueeze(1).broadcast_to([128, 2, 32])
                        tt("dve", rtmp[0][:, 0:2, :], k4[:, :, 0, :], cb2, ALU.mult, reads=[pdk, "ropeS_g"], writes=["rtmp0"])
                        tt("dve", rtmp[1][:, 0:2, :], k4[:, :, 1, :], sb2, ALU.mult, reads=[pdk, "ropeS_g"], writes=["rtmp1"])
                        tt("dve", rtmp[2][:, 0:2, :], k4[:, :, 1, :], cb2, ALU.mult, reads=[pdk, "ropeS_g"], writes=["rtmp2"])
                        tt("dve", rtmp[3][:, 0:2, :], k4[:, :, 0, :], sb2, ALU.mult, reads=[pdk, "ropeS_g"], writes=["rtmp3"])
                        tt("pool", kr_tm[:, :, 0, :], rtmp[0][:, 0:2, :], rtmp[1][:, 0:2, :], ALU.subtract, reads=["rtmp0", "rtmp1"], writes=[krk])
                        tt("pool", kr_tm[:, :, 1, :], rtmp[2][:, 0:2, :], rtmp[3][:, 0:2, :], ALU.add, reads=["rtmp2", "rtmp3"], writes=[krk])
                        slot = T % 8
                        tcopy("dve", Vs[:, slot, :, 0:64], pd[:, 128:256].rearrange("p (h c) -> p h c", c=64), reads=[pdk], writes=["Vs"])
                    def s2_B(t):
                        T = 4 * g + t
                        slot = T % 8
                        qr_tm, qrk = qr_tms[t % 2], f"qr_tm{t % 2}"
                        kr_tm, krk = kr_tms[t % 2], f"kr_tm{t % 2}"
                        for i in range(8):
                            tr(pT3[:, i, :], qr_tm[:, i, :, :], identb[:], reads=[qrk, "identb"], writes=["pT3"])
                        tcopy("dve", swa_qT[:, t, :, :], pT3[:], reads=["pT3"], writes=["bufA_q"])
                        tr(pT3[:, 0, :], kr_tm[:], identb[:], reads=[krk, "identb"], writes=["pT3"])
                        tcopy("dve", swa_kT[:, slot * 128:(slot + 1) * 128], pT3[:, 0, :], reads=["pT3"], writes=["swa_kT"])
                    s2_A(0)
                    for t in range(4):
                        if t + 1 < 4:
                            s2_A(t + 1)
                        s2_B(t)
                    ring_release(3)
                    def oslot(hh):
                        if hh < 7:
                            return pD[2], "pD2", hh * 65
                        if hh < 14:
                            return pD[2], "pD2", 512 + (hh - 7) * 65
                        return pX, "pX", (hh - 14) * 65

                    def ulist_of(t):
                        n = 4 * g + t
                        return ([(n - 1, maskP_sb, "maskP")] if n > 0 else []) + [(n, maskD_sb, "maskD")]

                    def s3_A(t, kvh):
                        ks = slice(kvh * 64, (kvh + 1) * 64)
                        for ui, (kb, msk, mskk) in enumerate(ulist_of(t)):
                            pd, pdk = pD[ui], f"pD{ui}"
                            sl = kb % 8
                            for half in range(2):
                                mm(pd[:, half * 512:(half + 1) * 512], swa_kT[ks, sl * 128:(sl + 1) * 128],
                                   swa_qT[ks, t, 4 * half:4 * half + 4, :], start=True, stop=False,
                                   reads=["swa_kT", "bufA_q"], writes=[pdk])
                                for i in range(4 * half, 4 * half + 4):
                                    mm(pd[:, i * 128:(i + 1) * 128], identb[:], msk[:], start=False, stop=(i % 4 == 3),
                                       reads=["identb", mskk], writes=[pdk])
                            ptile, ptk = PTs[kvh * 2 + ui], f"PTs{kvh * 2 + ui}"
                            act(ptile[:], pd[:], AF.Exp, reads=[pdk], writes=[ptk], scale=SC_SWA)

                    def s3_B(t, kvh):
                        ul = ulist_of(t)
                        for i in range(8):
                            hh = kvh * 8 + i
                            ot, otk, oc = oslot(hh)
                            for ui, (kb, msk, mskk) in enumerate(ul):
                                mm(ot[:, oc:oc + 65], PTs[kvh * 2 + ui][:, i * 128:(i + 1) * 128], Vs[:, kb % 8, kvh, :],
                                   start=(ui == 0), stop=(ui == len(ul) - 1), reads=[f"PTs{kvh * 2 + ui}", "Vs"], writes=[otk])

                    def s3_C(t):
                        tsl = slice(t * 128, (t + 1) * 128)
                        grp = [(pD[2], "pD2", 0, 0, 7), (pD[2], "pD2", 512, 7, 7), (pX, "pX", 0, 14, 2)]
                        for (ot, otk, base, h0, nh) in grp:
                            V3 = ot[:, base:base + nh * 65].rearrange("p (h c) -> p h c", c=65)
                            tt("dve", dn[:, h0:h0 + nh], V3[:, :, 64], esink[:, h0:h0 + nh], ALU.add, reads=[otk, "esink"], writes=["dn"])
                        P.op("dve", lambda hh_: hh_.reciprocal(out=dn[:], in_=dn[:]), reads=["dn"], writes=["dn"])
                        for (ot, otk, base, h0, nh) in grp:
                            V3 = ot[:, base:base + nh * 65].rearrange("p (h c) -> p h c", c=65)
                            tt("dve", o_tm[:, h0:h0 + nh, :], V3[:, :, 0:64], dn[:, h0:h0 + nh].unsqueeze(2).broadcast_to([128, nh, 64]), ALU.mult,
                               reads=[otk, "dn"], writes=["o_tm"])
                        for j in range(8):
                            tr(pT3[:, j, :], o_tm[:, 2 * j:2 * j + 2, :], identb[:], reads=["o_tm", "identb"], writes=["pT3"])
                        tcopy("dve", o_swaT[:, :, tsl], pT3[:], reads=["pT3"], writes=["bufA_o"])

                    s3_A(0, 0)
                    s3_A(0, 1)
                    for t in range(4):
                        s3_B(t, 0)
                        s3_B(t, 1)
                        if t + 1 < 4:
                            s3_A(t + 1, 0)
                            s3_A(t + 1, 1)
                        s3_C(t)
                    sA = uT[:, 0:4, :]
                    sB = uT[:, 4:8, :]
                    for half in range(2):
                        for (sdst, sk_) in ((sA, "uTa"), (sB, "uTb")):
                            wr, wrk = ring_get()
                            for jj in range(4):
                                pd, pdk = next_pd()
                                for kc in range(8):
                                    mm(pd[:, 0:512], wr[:, kc, jj * 128:(jj + 1) * 128], hT3[:, kc, :], start=(kc == 0), stop=(kc == 7),
                                       reads=[wrk, "hT3_0", "hT3_1", "hT3_2", "hT3_3"], writes=[pdk])
                                m_ = mt[jj % 2]
                                sigmoid_from(pd[:, 0:512], pdk, m_[:], f"mt{jj % 2}", sdst[:, jj, :], sk_)
                            ring_release(1)
                        wa, wak = ring_get()
                        wb, wbk = ring_get()
                        for jj in range(4):
                            pd, pdk = next_pd()
                            for kc in range(8):
                                mm(pd[:, 0:512], wa[:, kc, jj * 128:(jj + 1) * 128], o_mlaT[:, kc, gs], start=(kc == 0), stop=(kc == 7),
                                   reads=[wak, "o_mlaT"], writes=[pdk])
                            for kc in range(8):
                                mm(pd[:, 512:1024], wb[:, kc, jj * 128:(jj + 1) * 128], o_swaT[:, kc, :], start=(kc == 0), stop=(kc == 7),
                                   reads=[wbk, "bufA_o"], writes=[pdk])
                            tt("dve", mt[0][:], pd[:, 0:512], sA[:, jj, :], ALU.mult, reads=[pdk, "uTa"], writes=["mt0"])
                            tt("dve", mt[1][:], pd[:, 512:1024], sB[:, jj, :], ALU.mult, reads=[pdk, "uTb"], writes=["mt1"])
                            tt("pool", yT[:, half * 4 + jj, :], mt[0][:], mt[1][:], ALU.add, reads=["mt0", "mt1"], writes=["yT"])
                        ring_release(2)
                    w0, w0k = ring_get()
                    w1_, w1k = ring_get()
                    for t in range(4):
                        tsl = slice(t * 128, (t + 1) * 128)
                        pd, pdk = next_pd()
                        for ch, (wr, wrk) in enumerate(((w0, w0k), (w1_, w1k))):
                            for kc in range(8):
                                mm(pd[:, ch * 512:(ch + 1) * 512], yT[:, kc, tsl], wr[:, kc, 0:512], start=(kc == 0), stop=(kc == 7),
                                   reads=["yT", wrk], writes=[pdk])
                        rms_tile(pd[:], pdk, 4 + t)
                        stt("dve", stg2[:], pd[:], rs3[:, 4 + t:5 + t], grow_sb[:, 0, :], ALU.mult, ALU.mult,
                            reads=[pdk, f"rs3_{4 + t}", "grow", "rtmp0", "rtmp1", "rtmp2", "rtmp3"], writes=["rtmp0", "rtmp1", "rtmp2", "rtmp3"])
                        tt("dve", x_g[:, t, :], x_g[:, t, :], stg2[:], ALU.add, reads=[f"x_g{t}", "rtmp0", "rtmp1", "rtmp2", "rtmp3"], writes=[f"x_g{t}"])
                    ring_release(2)
                    for t in range(4):
                        rms_tile(x_g[:, t, :], f"x_g{t}", t)
                        to_hT(t, 8, x_g[:, t, :], f"x_g{t}", t)
                    for s_ in range(4):
                        for cidx in range(2):
                            wr, wrk = ring_get()
                            for jj in range(4):
                                fc = cidx * 4 + jj
                                pd, pdk = next_pd()
                                for kc in range(8):
                                    mm(pd[:, 0:512], wr[:, kc, jj * 128:(jj + 1) * 128], hT3[:, kc, :], start=(kc == 0), stop=(kc == 7),
                                       reads=[wrk, "hT3_0", "hT3_1", "hT3_2", "hT3_3"], writes=[pdk])
                                m_, mk_ = mt[jj % 2], f"mt{jj % 2}"
                                act(m_[:], pd[:, 0:512], AF.Relu, reads=[pdk], writes=[mk_])
                                uk = "uTa" if fc < 4 else "uTb"
                                tt("pool", uT[:, fc, :], m_[:], m_[:], ALU.mult, reads=[mk_], writes=[uk])
                            ring_release(1)
                        for ch in range(2):
                            wr, wrk = ring_get()
                            for t in range(4):
                                tsl = slice(t * 128, (t + 1) * 128)
                                pd, pdk = next_pd()
                                for fc in range(8):
                                    mm(pd[:, 0:512], uT[:, fc, tsl], wr[:, fc, 0:512], start=(fc == 0), stop=(fc == 7),
                                       reads=["uTa", "uTb", wrk], writes=[pdk])
                                ak = "bufA_q" if t < 2 else "bufA_o"
                                dst = x2acc[:, t, ch * 512:(ch + 1) * 512]
                                if s_ == 0:
                                    tcopy("dve", dst, pd[:, 0:512], reads=[pdk], writes=[ak])
                                else:
                                    tt("dve", dst, pd[:, 0:512], dst, ALU.add, reads=[pdk, ak], writes=[ak])
                            ring_release(1)
                    for t in range(4):
                        ak = "bufA_q" if t < 2 else "bufA_o"
                        rms_tile(x2acc[:, t, :], ak, 4 + t)
                        stt("dve", stg2[:], x2acc[:, t, :], rs3[:, 4 + t:5 + t], grow_sb[:, 1, :], ALU.mult, ALU.mult,
                            reads=[ak, f"rs3_{4 + t}", "grow", "rtmp0", "rtmp1", "rtmp2", "rtmp3"], writes=["rtmp0", "rtmp1", "rtmp2", "rtmp3"])
                        tt("dve", x_g[:, t, :], x_g[:, t, :], stg2[:], ALU.add, reads=[f"x_g{t}", "rtmp0", "rtmp1", "rtmp2", "rtmp3"], writes=[f"x_g{t}"])
                    def s8_A(t):
                        T = 4 * g + t
                        to_hT(t, None, x_g[:, t, :], f"x_g{t}", None)
                        ptb, ptk_ = pt_sbs[t % 2], f"pt_sb{t % 2}"
                        dma("sp", f"d_p{t % 2}", ptb[:], p_in[T * 128:(T + 1) * 128, :], writes=[ptk_])
                        tcopy("dve", pn_sb[:], ptb[:], reads=[ptk_], writes=["pn_sb"])
                        for kc in range(2):
                            tr(pT3[:, kc, :], pn_sb[:, kc * 128:(kc + 1) * 128], identb[:], reads=["pn_sb", "identb"], writes=["pT3"])
                        tcopy("dve", pT_sb[:, :, t * 128:(t + 1) * 128], pT3[:, 0:2, :], reads=["pT3"], writes=[f"pT_sb{t}"])

                    w0, w0k = ring_get()
                    w1_, w1k = ring_get()

                    def s8_B(t):
                        T = 4 * g + t
                        tsl = slice(t * 128, (t + 1) * 128)
                        pa, pak = next_pd()
                        pb_, pbk = next_pd()
                        for ch, (wr, wrk) in enumerate(((w0, w0k), (w1_, w1k))):
                            for kc in range(8):
                                mm(pa[:, ch * 512:(ch + 1) * 512], hT3[:, kc, tsl], wr[:, kc, 0:512], start=(kc == 0), stop=(kc == 7),
                                   reads=[f"hT3_{t}", wrk], writes=[pak])
                            for kc in range(2):
                                mm(pb_[:, ch * 512:(ch + 1) * 512], pT_sb[:, kc, tsl], wpp[:, kc, ch * 512:(ch + 1) * 512], start=(kc == 0), stop=(kc == 1),
                                   reads=[f"pT_sb{t}", "wpp"], writes=[pbk])
                        so = stg[st_i[0] % 2]
                        sok = f"stg{st_i[0] % 2}"
                        st_i[0] += 1
                        sigmoid_from(pa[:], pak, so[:], sok, so[:], sok)
                        rms_tile(pb_[:], pbk, 4 + t)
                        stt("dve", stg2[:], pb_[:], rs3[:, 4 + t:5 + t], grow_sb[:, 2, :], ALU.mult, ALU.mult,
                            reads=[pbk, f"rs3_{4 + t}", "grow", "rtmp0", "rtmp1", "rtmp2", "rtmp3"], writes=["rtmp0", "rtmp1", "rtmp2", "rtmp3"])
                        tt("dve", so[:], so[:], stg2[:], ALU.mult, reads=[sok, "rtmp0", "rtmp1", "rtmp2", "rtmp3"], writes=[sok])
                        tt("dve", so[:], so[:], x_g[:, t, :], ALU.add, reads=[sok, f"x_g{t}"], writes=[sok])
                        dma("pool", f"d_out{st_i[0] % 2}", out[T * 128:(T + 1) * 128, :], so[:], reads=[sok], writes=["out"])

                    s8_A(0)
                    for t in range(4):
                        if t + 1 < 4:
                            s8_A(t + 1)
                        s8_B(t)
                        if g + 1 < NG and t >= 1:
                            s1_tile(g + 1, t - 1)
                    if g + 1 < NG:
                        s1_tile(g + 1, 3)
                    ring_release(2)
                P.barrier()
        P.barrier(final=True)
        P.emit()
    return nc


def _rope_tables():
    pos = np.arange(S, dtype=np.float64)
    inv16 = np.exp(-math.log(10000.0) * np.arange(16, dtype=np.float64) * (2.0 / 32))
    ang16 = pos[:, None] * inv16[None, :]
    c16, s16 = np.cos(ang16), np.sin(ang16)
    ck = np.concatenate([c16, c16, s16, s16], axis=1).astype(np.float32)
    ropeK = ck.reshape(NT, 128, 64).transpose(1, 0, 2).copy()
    ropeQ = np.zeros((2, 128, S), np.float32)
    ropeQ[0, 64:96, :] = np.concatenate([c16, c16], axis=1).T.astype(np.float32)
    ropeQ[1, 96:128, :] = np.concatenate([s16, s16], axis=1).T.astype(np.float32)
    inv32 = np.exp(-math.log(10000.0) * np.arange(32, dtype=np.float64) * (2.0 / 64))
    ang32 = pos[:, None] * inv32[None, :]
    cs = np.concatenate([np.cos(ang32), np.sin(ang32)], axis=1).astype(np.float32)
    ropeS = cs.reshape(NT, 128, 64).transpose(1, 0, 2).copy()
    return ropeK, ropeS, ropeQ


def make_in_maps(inputs):
    f = lambda a: np.ascontiguousarray(np.asarray(a, dtype=np.float32))
    x = f(inputs["x"])
    p = f(inputs["p"])[0]
    gmp, gqa, gkv = f(inputs["g_mix_pre"])[0], f(inputs["g_q_a"])[0], f(inputs["g_kv_a"])[0]
    gmlp = f(inputs["g_mlp_pre"])[0]
    gcol = np.concatenate([gmp.reshape(8, 128).T, gmlp.reshape(8, 128).T, gqa.reshape(2, 128).T, gkv.reshape(1, 128).T], axis=1)
    grow = np.stack([np.broadcast_to(f(inputs[k])[0][None, :], (128, 1024)) for k in ("g_mix_post", "g_mlp_post", "g_ple")], axis=1)
    sinks_b = np.broadcast_to(f(inputs["sinks"])[0][None, :], (128, 16))
    ropeK, ropeS, ropeQ = _rope_tables()
    kk = np.arange(128)
    NEGB = np.float32(-30000.0)
    maskD = np.where(kk[:, None] <= kk[None, :], np.float32(0), NEGB).astype(ml_dtypes.bfloat16)
    maskP = np.where(kk[:, None] > kk[None, :], np.float32(0), NEGB).astype(ml_dtypes.bfloat16)
    common = {
        "w_in": f(inputs["w_in"])[0], "w_q_b": f(inputs["w_q_b"])[0], "w_kv_b": f(inputs["w_kv_b"])[0],
        "w_mla_up": f(inputs["w_mla_up"])[0], "w_swa_up": f(inputs["w_swa_up"])[0], "w_out": f(inputs["w_out"])[0],
        "w_mlp_up": f(inputs["w_mlp_up"])[0], "w_mlp_down": f(inputs["w_mlp_down"])[0],
        "w_ple": f(inputs["w_ple"])[0], "w_ple_gate": f(inputs["w_ple_gate"])[0],
        "gcol": np.ascontiguousarray(gcol), "grow": np.ascontiguousarray(grow), "sinks_b": np.ascontiguousarray(sinks_b),
        "ident": np.eye(128, dtype=np.float32).astype(ml_dtypes.bfloat16), "maskD": maskD, "maskP": maskP,
        "ropeK": ropeK, "ropeS": ropeS, "ropeQ": ropeQ,
    }
    return [dict(common, x=np.ascontiguousarray(x[b]), p=np.ascontiguousarray(p[b])) for b in range(x.shape[0])]


def kernel(**inputs):
    in_maps = make_in_maps(inputs)
    nc = build()
    res = run_bass_kernel_spmd(nc, in_maps, core_ids=list(range(8)))
    return np.stack([r["out"] for r in res.results], axis=0).astype(np.float32)
```

```python
import math
import os
import numpy as np
from contextlib import ExitStack
import ml_dtypes
import concourse.bass as bass
import concourse.mybir as mybir
from concourse.bass_utils import run_bass_kernel_spmd

F32 = mybir.dt.float32
BF16 = mybir.dt.bfloat16
ALU = mybir.AluOpType
AF = mybir.ActivationFunctionType

ENGS = ["pe", "act", "dve", "pool", "sp"]

S = 4096
D = 1024
NT = 32
NG = 8
SC_MLA = 96 ** -0.5
SC_SWA = 64 ** -0.5
EPS = 1e-6


class Prog:
    def __init__(self, nc, stack):
        self.nc = nc
        self.stack = stack
        self.q = {e: [] for e in ENGS}
        self.cnt = {}
        self.seen = {e: {} for e in ENGS}
        self.sems = {}
        self.lastw = {}
        self.readers = {}
        for e in ENGS:
            self._sem(e)

    def _sem(self, name):
        if name not in self.sems:
            self.sems[name] = self.stack.enter_context(self.nc.semaphore("s_" + name))
            self.cnt[name] = 0
        return self.sems[name]

    def _wait(self, eng, s, v):
        if eng == "pe" and s == "pe":
            return
        if self.seen[eng].get(s, 0) < v:
            self.seen[eng][s] = v
            sem = self.sems[s]
            self.q[eng].append(lambda h, sem=sem, v=v: h.wait_ge(sem, v))

    def _deps(self, eng, reads, writes):
        need = {}

        def add(t):
            if t is None:
                return
            s, v = t
            if need.get(s, 0) < v:
                need[s] = v

        for r in reads:
            add(self.lastw.get(r))
        for w in writes:
            add(self.lastw.get(w))
            for t in self.readers.get(w, ()):
                add(t)
        for s, v in need.items():
            self._wait(eng, s, v)

    def _commit(self, ticket, reads, writes):
        for r in reads:
            lst = self.readers.setdefault(r, [])
            lst[:] = [t for t in lst if t[0] != ticket[0]]
            lst.append(ticket)
        for w in writes:
            self.lastw[w] = ticket
            self.readers[w] = []

    def op(self, eng, fn, reads=(), writes=()):
        reads = list(reads)
        writes = list(writes)
        self._deps(eng, reads, writes)
        self.cnt[eng] += 1
        sem = self.sems[eng]
        self.q[eng].append(lambda h, sem=sem, fn=fn: fn(h).then_inc(sem, 1))
        t = (eng, self.cnt[eng])
        self._commit(t, reads, writes)
        return t

    def dma(self, eng, semname, fn, reads=(), writes=()):
        reads = list(reads)
        writes = list(writes)
        sem = self._sem(semname)
        self._deps(eng, reads, writes)
        self.cnt[semname] += 16
        self.q[eng].append(lambda h, sem=sem, fn=fn: fn(h).then_inc(sem, 16))
        t = (semname, self.cnt[semname])
        self._commit(t, reads, writes)
        return t

    def wait_all(self, eng, keys):
        self._deps(eng, list(keys), [])

    def barrier(self, final=False):
        keep = self.lastw.get("scratch")
        for e in ENGS:
            for s, c in self.cnt.items():
                if c > 0 and (final or s != "d_scr"):
                    self._wait(e, s, c)
        self.lastw.clear()
        self.readers.clear()
        if keep is not None and not final:
            self.lastw["scratch"] = keep

    def emit(self):
        nc = self.nc
        q = self.q
        with nc.Block() as block:
            @block.tensor
            def _(h):
                for f in q["pe"]:
                    f(h)

            @block.scalar
            def _(h):
                for f in q["act"]:
                    f(h)

            @block.vector
            def _(h):
                for f in q["dve"]:
                    f(h)

            @block.gpsimd
            def _(h):
                for f in q["pool"]:
                    f(h)

            @block.sync
            def _(h):
                for f in q["sp"]:
                    f(h)


def build(stage=3, dbg=()):
    nc = bass.Bass("TRN2", target_bir_lowering=False)

    def din(name, shape, dt=F32):
        return nc.dram_tensor(name, list(shape), dt, kind="ExternalInput").ap()

    x = din("x", [S, D])
    p_in = din("p", [S, 256])
    w_in = din("w_in", [1024, 3744])
    w_q_b = din("w_q_b", [256, 1536])
    w_kv_b = din("w_kv_b", [128, 2048])
    w_mla_up = din("w_mla_up", [1024, 1024])
    w_swa_up = din("w_swa_up", [1024, 1024])
    w_out = din("w_out", [1024, 1024])
    w_mlp_up = din("w_mlp_up", [1024, 4096])
    w_mlp_down = din("w_mlp_down", [4096, 1024])
    w_ple = din("w_ple", [256, 1024])
    w_ple_gate = din("w_ple_gate", [1024, 1024])
    gcol = din("gcol", [128, 19])
    grow = din("grow", [128, 3, 1024])
    sinks_b = din("sinks_b", [128, 16])
    ident = din("ident", [128, 128], BF16)
    maskD = din("maskD", [128, 128], BF16)
    maskP = din("maskP", [128, 128], BF16)
    ropeK = din("ropeK", [128, 32, 64])
    ropeS = din("ropeS", [128, 32, 64])
    ropeQ = din("ropeQ", [2, 128, S])
    out = nc.dram_tensor("out", [S, D], F32, kind="ExternalOutput").ap()
    dbg_t = {}
    for name, shape in dbg:
        dbg_t[name] = nc.dram_tensor("dbg_" + name, list(shape), F32, kind="ExternalOutput").ap()

    sc_in = nc.dram_tensor("sc_in", [1024, 3744], BF16).ap()
    sc_upA = nc.dram_tensor("sc_upA", [1024, 1024], BF16).ap()
    sc_upB = nc.dram_tensor("sc_upB", [1024, 1024], BF16).ap()
    sc_out = nc.dram_tensor("sc_out", [1024, 1024], BF16).ap()
    sc_w1 = nc.dram_tensor("sc_w1", [1024, 4096], BF16).ap()
    sc_w2 = nc.dram_tensor("sc_w2", [4096, 1024], BF16).ap()
    sc_pg = nc.dram_tensor("sc_pg", [1024, 1024], BF16).ap()

    def sbuf(st, n, s, d):
        return st.enter_context(nc.sbuf_tensor(n, list(s), d))

    def psum(st, n, s, d):
        return st.enter_context(nc.psum_tensor(n, list(s), d))

    with ExitStack() as st0:
        P = Prog(nc, st0)

        def mm(out_, lhsT, rhs, start=True, stop=True, reads=(), writes=()):
            return P.op("pe", lambda h: h.matmul(out_, lhsT=lhsT, rhs=rhs, start=start, stop=stop), reads, writes)

        def tr(out_, in_, idn, reads=(), writes=()):
            return P.op("pe", lambda h: h.transpose(out=out_, in_=in_, identity=idn), reads, writes)

        def act(out_, in_, func, reads=(), writes=(), eng="act", **kw):
            return P.op(eng, lambda h: h.activation(out=out_, in_=in_, func=func, **kw), reads, writes)

        def tcopy(eng, out_, in_, reads=(), writes=()):
            if eng == "act":
                return P.op(eng, lambda h: h.copy(out=out_, in_=in_), reads, writes)
            return P.op(eng, lambda h: h.tensor_copy(out=out_, in_=in_), reads, writes)

        def tt(eng, out_, in0, in1, op, reads=(), writes=()):
            return P.op(eng, lambda h: h.tensor_tensor(out=out_, in0=in0, in1=in1, op=op), reads, writes)

        def ts(eng, out_, in0, s1, s2, op0, op1=None, reads=(), writes=(), **kw):
            if op1 is None:
                return P.op(eng, lambda h: h.tensor_scalar(out=out_, in0=in0, scalar1=s1, scalar2=None, op0=op0, **kw), reads, writes)
            return P.op(eng, lambda h: h.tensor_scalar(out=out_, in0=in0, scalar1=s1, scalar2=s2, op0=op0, op1=op1, **kw), reads, writes)

        def stt(eng, out_, in0, scalar, in1, op0, op1, reads=(), writes=()):
            return P.op(eng, lambda h: h.scalar_tensor_tensor(out=out_, in0=in0, scalar=scalar, in1=in1, op0=op0, op1=op1), reads, writes)

        def memset(eng, ap, val, writes=()):
            return P.op(eng, lambda h: h.memset(ap, val), (), writes)

        uniq = [0]

        def dma(eng, sem, out_, in_, reads=(), writes=()):
            if sem in ("d_c", "d_w2"):
                uniq[0] += 1
                sem = f"d_u{uniq[0]}"
            return P.dma(eng, sem, lambda h: h.dma_start(out=out_, in_=in_), reads, writes)

        def dump(name, src_ap, key):
            if name in dbg_t:
                dma("sp", "d_dbg", dbg_t[name], src_ap, reads=[key], writes=["dbg_" + name])

        o_mlaT = sbuf(st0, "o_mlaT", [128, 8, S], BF16)
        identb = sbuf(st0, "identb", [128, 128], BF16)
        maskD_sb = sbuf(st0, "maskD_sb", [128, 128], BF16)
        maskP_sb = sbuf(st0, "maskP_sb", [128, 128], BF16)
        gcol_sb = sbuf(st0, "gcol_sb", [128, 19], F32)
        epsb = sbuf(st0, "epsb", [128, 1], F32)
        ones32 = sbuf(st0, "ones32", [128, 128], F32)

        dma("sp", "d_c", identb[:], ident, writes=["identb"])
        dma("sp", "d_c", maskD_sb[:], maskD, writes=["maskD"])
        dma("sp", "d_c", maskP_sb[:], maskP, writes=["maskP"])
        dma("sp", "d_c", gcol_sb[:], gcol, writes=["gcol"])
        memset("pool", epsb[:], EPS, writes=["epsb"])
        memset("pool", ones32[:], 1.0, writes=["ones32"])

        def rstd_from_ss(ss_ap, out_ap, lnv_ap, inv_n, rkey, wkey, lkey):
            act(lnv_ap, ss_ap, AF.Ln, reads=[rkey, "epsb"], writes=[lkey], scale=inv_n, bias=epsb[:, 0:1])
            act(out_ap, lnv_ap, AF.Exp, reads=[lkey], writes=[wkey], scale=-0.5)

        cast_jobs = []

        def cast_rows(dst, src, rows):
            for r0 in range(0, rows, 128):
                cast_jobs.append((dst[r0:r0 + 128, :], src[r0:r0 + 128, :]))

        def cast_emit(n=1):
            for _ in range(n):
                if cast_jobs:
                    d_, s_ = cast_jobs.pop(0)
                    dma("pool", "d_scr", d_, s_, writes=["scratch"])

        with ExitStack() as st12:
            qaT = sbuf(st12, "qaT", [128, 2, S], BF16)
            kvaT = sbuf(st12, "kvaT", [128, S], BF16)
            kT = [sbuf(st12, f"kT{b}", [96, S], BF16) for b in range(2)]
            wqx = sbuf(st12, "wqx", [128, 2, 16, 128], BF16)
            wkvb = sbuf(st12, "wkvb", [128, 2048], BF16)

            with ExitStack() as st1:
                NB1 = 6
                w_g0 = sbuf(st1, "w_g0", [128, 8, 448], BF16)
                ropeK_sb = sbuf(st1, "ropeK_sb", [128, 32, 64], F32)
                xt = [sbuf(st1, f"p1xt{i}", [128, D], F32) for i in range(NB1)]
                junk = [sbuf(st1, f"p1junk{i}", [128, D], BF16) for i in range(NB1)]
                st_sb = [sbuf(st1, f"p1st_sb{i}", [128, 8], F32) for i in range(NB1)]
                xn = [sbuf(st1, f"p1xn{i}", [128, D], BF16) for i in range(NB1)]
                hT = [sbuf(st1, f"p1hT{i}", [128, 8, 128], BF16) for i in range(NB1)]
                qan = [sbuf(st1, f"p1qan{i}", [128, 256], BF16) for i in range(NB1)]
                kvan = [sbuf(st1, f"p1kvan{i}", [128, 128], BF16) for i in range(NB1)]
                krm = [sbuf(st1, f"p1krm{i}", [128, 128], BF16) for i in range(NB1)]
                kt12 = [sbuf(st1, f"p1kt12{i}", [128, 64], F32) for i in range(NB1)]
                pT = [psum(st1, f"pT{i}", [128, 8, 128], BF16) for i in range(2)]
                pG0 = [psum(st1, f"pG0{i}", [128, 512], F32) for i in range(3)]
                pTB = [psum(st1, f"pTB{i}", [128, 8, 128], BF16) for i in range(2)]

                dma("pool", "d_w1", w_g0[:, :, 0:416], w_in[:, 0:416].rearrange("(kc p) c -> p kc c", p=128), writes=["w_g0"])
                dma("sp", "d_c", ropeK_sb[:], ropeK, writes=["ropeK"])
                ts("pool", w_g0[:, :, 416:432], w_g0[:, :, 400:416], -1.0, None, ALU.mult, reads=["w_g0"], writes=["w_g0r"])
                tcopy("pool", w_g0[:, :, 432:448], w_g0[:, :, 384:400], reads=["w_g0"], writes=["w_g0r2"])
                for i in range(NB1):
                    memset("pool", krm[i][:], 0.0, writes=[f"krm{i}"])
                for kc_ in range(2):
                    dma("pool", "d_w2", wqx[:, kc_, :, 0:96], w_q_b[kc_ * 128:(kc_ + 1) * 128, :].rearrange("p (h c) -> p h c", c=96), writes=[f"wqb{kc_}"])
                dma("pool", "d_w2", wkvb[:], w_kv_b, writes=["wkvb"])
                ts("pool", wqx[:, :, :, 96:112], wqx[:, :, :, 80:96], -1.0, None, ALU.mult, reads=["wqb0", "wqb1"], writes=["wqrot"])
                tcopy("pool", wqx[:, :, :, 112:128], wqx[:, :, :, 64:80], reads=["wqb0", "wqb1"], writes=["wqrot2"])

                if stage >= 3:
                    cast_rows(sc_in, w_in, 1024)
                    cast_rows(sc_upA, w_mla_up, 1024)
                    cast_rows(sc_upB, w_swa_up, 1024)
                    cast_rows(sc_out, w_out, 1024)
                    cast_rows(sc_w1, w_mlp_up, 1024)
                    cast_rows(sc_w2, w_mlp_down, 4096)
                    cast_rows(sc_pg, w_ple_gate, 1024)

                def p1_stage(t, stage_):
                    i = t % NB1
                    K_ = lambda n: f"{n}{i}"
                    xb, jk, sb_, xnb, hTb = xt[i], junk[i], st_sb[i], xn[i], hT[i]
                    pTt, pTk = pT[t % 2], f"pT{t % 2}"
                    pG, pGk = pG0[t % 3], f"pG0{t % 3}"
                    pTBt, pTBk = pTB[t % 2], f"pTB{t % 2}"
                    if stage_ == 0:
                        dma("sp", f"d_x{i}", xb[:], x[t * 128:(t + 1) * 128, :], writes=[K_("xt")])
                        act(jk[:], xb[:], AF.Square, reads=[K_("xt")], writes=[K_("junk"), K_("ss")], accum_out=sb_[:, 0:1])
                        rstd_from_ss(sb_[:, 0:1], sb_[:, 2:3], sb_[:, 1:2], 1.0 / D, K_("ss"), K_("rstd"), K_("lnv"))
                        ts("dve", xnb[:], xb[:], sb_[:, 2:3], None, ALU.mult, reads=[K_("xt"), K_("rstd")], writes=[K_("xn")])
                    elif stage_ == 1:
                        for kc in range(8):
                            tr(pTt[:, kc, :], xnb[:, kc * 128:(kc + 1) * 128], identb[:], reads=[K_("xn"), "identb"], writes=[pTk])
                        tt("dve", hTb[:], pTt[:], gcol_sb[:, 0:8].unsqueeze(2).broadcast_to([128, 8, 128]), ALU.mult,
                           reads=[pTk, "gcol"], writes=[K_("hT")])
                    elif stage_ == 2:
                        for kc in range(8):
                            mm(pG[:, 0:448], hTb[:, kc, :], w_g0[:, kc, :], start=(kc == 0), stop=(kc == 7),
                               reads=[K_("hT"), "w_g0", "w_g0r", "w_g0r2"], writes=[pGk])
                        act(jk[:, 0:256], pG[:, 0:256], AF.Square, reads=[pGk], writes=[K_("junk"), K_("ss2a")], accum_out=sb_[:, 3:4])
                        act(jk[:, 256:384], pG[:, 256:384], AF.Square, reads=[pGk], writes=[K_("junk"), K_("ss2b")], accum_out=sb_[:, 4:5])
                        rstd_from_ss(sb_[:, 3:4], sb_[:, 5:6], sb_[:, 6:7], 1.0 / 256, K_("ss2a"), K_("rs2a"), K_("lnv2a"))
                        rstd_from_ss(sb_[:, 4:5], sb_[:, 7:8], sb_[:, 6:7], 1.0 / 128, K_("ss2b"), K_("rs2b"), K_("lnv2a"))
                    elif stage_ == 3:
                        ts("dve", qan[i][:], pG[:, 0:256], sb_[:, 5:6], None, ALU.mult, reads=[pGk, K_("rs2a")], writes=[K_("qan")])
                        ts("dve", kvan[i][:], pG[:, 256:384], sb_[:, 7:8], None, ALU.mult, reads=[pGk, K_("rs2b")], writes=[K_("kvan")])
                        tt("dve", kt12[i][:], pG[:, 384:448], ropeK_sb[:, t, :], ALU.mult, reads=[pGk, "ropeK"], writes=[K_("kt12")])
                        tt("dve", krm[i][:, 64:96], kt12[i][:, 0:32], kt12[i][:, 32:64], ALU.add, reads=[K_("kt12")], writes=[K_("krm")])
                    else:
                        tr(pTBt[:, 0, :], qan[i][:, 0:128], identb[:], reads=[K_("qan"), "identb"], writes=[pTBk])
                        tr(pTBt[:, 1, :], qan[i][:, 128:256], identb[:], reads=[K_("qan"), "identb"], writes=[pTBk])
                        tr(pTBt[:, 2, :], kvan[i][:], identb[:], reads=[K_("kvan"), "identb"], writes=[pTBk])
                        tr(pTBt[:, 3, :], krm[i][:], identb[:], reads=[K_("krm"), "identb"], writes=[pTBk])
                        tsl = slice(t * 128, (t + 1) * 128)
                        tt("dve", qaT[:, :, tsl], pTBt[:, 0:2, :], gcol_sb[:, 16:18].unsqueeze(2).broadcast_to([128, 2, 128]), ALU.mult,
                           reads=[pTBk, "gcol"], writes=["qaT"])
                        ts("dve", kvaT[:, tsl], pTBt[:, 2, :], gcol_sb[:, 18:19], None, ALU.mult, reads=[pTBk, "gcol"], writes=["kvaT"])
                        tcopy("dve", kT[0][64:96, tsl], pTBt[64:96, 3, :], reads=[pTBk], writes=["kT0r"])
                        tcopy("dve", kT[1][64:96, tsl], pTBt[64:96, 3, :], reads=[pTBk], writes=["kT1r"])

                NST = 5
                for it in range(NT + NST - 1):
                    for k in range(NST):
                        if 0 <= it - k < NT:
                            p1_stage(it - k, k)
                P.barrier()
                if "qaT" in dbg_t:
                    for kc in range(2):
                        for c in range(4):
                            tcopy("dve", xt[0][:], qaT[:, kc, c * 1024:(c + 1) * 1024], reads=["qaT"], writes=["xt0"])
                            dma("sp", "d_dbg", dbg_t["qaT"][kc, :, c * 1024:(c + 1) * 1024], xt[0][:], reads=["xt0"], writes=["dbg"])
                    for c in range(4):
                        tcopy("dve", xt[0][0:96, :], kT[0][:, c * 1024:(c + 1) * 1024], reads=["qaT"], writes=["xt0"])
                        dma("sp", "d_dbg", dbg_t["kT"][:, c * 1024:(c + 1) * 1024], xt[0][0:96, :], reads=["xt0"], writes=["dbg"])
                    P.barrier()

            if stage >= 2:
                with ExitStack() as st2:
                    cosT = sbuf(st2, "cosT", [128, S], F32)
                    sinT = sbuf(st2, "sinT", [128, S], F32)
                    Vp = [sbuf(st2, f"Vp{b}", [128, 32, 128], BF16) for b in range(2)]
                    qT = [sbuf(st2, f"qT{b}", [96, S], BF16) for b in range(2)]
                    NPT = 6
                    PT = [sbuf(st2, f"PT{i}", [128, 512], BF16) for i in range(NPT)]
                    rt1 = sbuf(st2, "rt1", [128, 512], F32)
                    rt2 = sbuf(st2, "rt2", [128, 512], F32)
                    recrow = [sbuf(st2, f"recrow{i}", [128, 512], F32) for i in range(2)]

                    NPS = 3
                    pS = [psum(st2, f"pS{i}", [128, 512], F32) for i in range(NPS)]
                    NPO = 3
                    pO = [psum(st2, f"pO{i}", [128, 512], F32) for i in range(NPO)]
                    pgen = [psum(st2, f"pgen{i}", [128, 512], F32) for i in range(2)]

                    dma("sp", "d_c", cosT[:], ropeQ[0], writes=["cosT"])
                    dma("sp", "d_c", sinT[:], ropeQ[1], writes=["sinT"])
                    for b in range(2):
                        memset("pool", Vp[b][:], 1.0, writes=[f"Vp{b}"])

                    gen_i = [0]

                    def gen_steps(h):
                        b = h % 2
                        steps = []

                        def k_step(c):
                            pg = pgen[gen_i[0] % 2]
                            pk = f"pgen{gen_i[0] % 2}"
                            gen_i[0] += 1
                            mm(pg[0:64, :], wkvb[:, h * 128:h * 128 + 64], kvaT[:, c * 512:(c + 1) * 512],
                               reads=["wkvb"], writes=[pk])
                            tcopy("dve", kT[b][0:64, c * 512:(c + 1) * 512], pg[0:64, :], reads=[pk], writes=[f"kT{b}"])

                        def v_step(tg):
                            pg = pgen[gen_i[0] % 2]
                            pk = f"pgen{gen_i[0] % 2}"
                            gen_i[0] += 1
                            for i in range(8):
                                tl = tg * 8 + i
                                mm(pg[:, i * 64:(i + 1) * 64], kvaT[:, tl * 128:(tl + 1) * 128], wkvb[:, h * 128 + 64:h * 128 + 128],
                                   reads=["wkvb"], writes=[pk])
                            tcopy("dve", Vp[b][:, tg * 8:(tg + 1) * 8, 0:64], pg[:].rearrange("p (i c) -> p i c", c=64),
                                  reads=[pk], writes=[f"Vp{b}"])

                        def q_step(c):
                            cs = slice(c * 512, (c + 1) * 512)
                            pq = pgen[gen_i[0] % 2]
                            pqk = f"pgen{gen_i[0] % 2}"
                            gen_i[0] += 1
                            for kc in range(2):
                                mm(pq[:, :], wqx[:, kc, h, :], qaT[:, kc, cs], start=(kc == 0), stop=(kc == 1),
                                   reads=["wqb0", "wqb1", "wqrot", "wqrot2"], writes=[pqk])
                            tcopy("dve", qT[b][0:64, cs], pq[0:64, :], reads=[pqk], writes=[f"qT{b}"])
                            tt("dve", rt1[64:96, :], pq[64:96, :], cosT[64:96, cs], ALU.mult, reads=[pqk, "cosT"], writes=["rt1"])
                            tt("dve", rt2[64:96, :], pq[96:128, :], sinT[96:128, cs], ALU.mult, reads=[pqk, "sinT"], writes=["rt2"])
                            tt("pool", qT[b][64:96, cs], rt1[64:96, :], rt2[64:96, :], ALU.add, reads=["rt1", "rt2"], writes=[f"qT{b}"])

                        for c in range(8):
                            steps.append(lambda c=c: k_step(c))
                            if c % 2 == 0:
                                steps.append(lambda tg=c // 2: v_step(tg))
                            steps.append(lambda c=c: q_step(c))
                        return steps

                    def gen_head(h):
                        for f in gen_steps(h):
                            f()

                    units = []
                    for h in range(16):
                        for c in range(8):
                            nkb = 4 * c + 4
                            for kb in range(nkb):
                                q0 = (kb - 4 * c) * 128 if kb >= 4 * c else 0
                                units.append((h, c, kb, q0, kb == 0, kb == nkb - 1, nkb))
                    LOOK = 2
                    pending = []
                    infos = {}

                    def qk(j):
                        h, c, kb, q0, first, last, nkb = units[j]
                        b = h % 2
                        ps_, psk = pS[j % NPS], f"pS{j % NPS}"
                        pt_, ptk = PT[j % NPT], f"PT{j % NPT}"
                        diag = kb >= 4 * c
                        kblk = kT[b][0:96, kb * 128:(kb + 1) * 128]
                        rk = [f"kT{b}", f"kT{b}r", f"qT{b}"]
                        mm(ps_[:, q0:512], kblk, qT[b][0:96, c * 512 + q0:(c + 1) * 512],
                           start=True, stop=(not diag), reads=rk, writes=[psk])
                        if diag:
                            mm(ps_[:, q0:q0 + 128], identb[:], maskD_sb[:], start=False, stop=True,
                               reads=["identb", "maskD"], writes=[psk])
                        act(pt_[:, q0:512], ps_[:, q0:512], AF.Exp, reads=[psk], writes=[ptk], scale=SC_MLA)
                        infos[j] = (pt_, ptk)

                    def pv(j):
                        h, c, kb, q0, first, last, nkb = units[j]
                        b = h % 2
                        ci = h * 8 + c
                        po, pok = pO[ci % NPO], f"pO{ci % NPO}"
                        pt_, ptk = infos.pop(j)
                        mm(po[:, q0:512], Vp[b][:, kb, :], pt_[:, q0:512], start=first, stop=last,
                           reads=[f"Vp{b}", ptk], writes=[pok])
                        if last:
                            rr, rrk = recrow[ci % 2], f"recrow{ci % 2}"
                            P.op("dve", lambda hh, rr=rr, po=po: hh.reciprocal(out=rr[64:128, :], in_=po[64:128, :]), reads=[pok], writes=[rrk])
                            hp, hs = h // 2, (h % 2) * 64
                            tt("dve", o_mlaT[hs:hs + 64, hp, c * 512:(c + 1) * 512], po[0:64, :], rr[64:128, :], ALU.mult,
                               reads=[pok, rrk], writes=["o_mlaT"])
                        return None

                    gen_head(0)
                    gsteps = []
                    NU = len(units)
                    for j in range(NU + LOOK):
                        if j < NU:
                            h, c, kb = units[j][0], units[j][1], units[j][2]
                            if c == 0 and kb == 0:
                                while gsteps:
                                    gsteps.pop(0)()
                                if h + 1 < 16:
                                    gsteps.extend(gen_steps(h + 1))
                            elif gsteps and j % 6 == 0:
                                gsteps.pop(0)()
                            if j % 20 == 10:
                                cast_emit(1)
                            qk(j)
                        if j >= LOOK:
                            f = pv(j - LOOK)
                            if f is not None:
                                jn = j - LOOK + 1
                                n_next = units[jn][6] if jn < NU else 0
                                pending.append((j + max(0, min(6, n_next - 1)), f))
                        while pending and pending[0][0] <= j:
                            pending.pop(0)[1]()
                    while pending:
                        pending.pop(0)[1]()
                    cast_emit(len(cast_jobs))
                    P.barrier()
                    if "o_mlaT" in dbg_t:
                        for j in range(8):
                            for c in range(8):
                                tcopy("dve", rt1[:], o_mlaT[:, j, c * 512:(c + 1) * 512], reads=["o_mlaT"], writes=["rt1"])
                                dma("sp", "d_dbg", dbg_t["o_mlaT"][j, :, c * 512:(c + 1) * 512], rt1[:], reads=["rt1"], writes=["dbg"])
                        P.barrier()
        P.barrier()
        if stage >= 3:
            with ExitStack() as st3:
                x_g = sbuf(st3, "x_g", [128, 4, D], F32)
                hT3 = sbuf(st3, "hT3", [128, 8, 512], BF16)
                bufA = sbuf(st3, "bufA", [128, 8192], BF16)
                swa_qT = bufA[:, 0:4096].rearrange("p (t j q) -> p t j q", j=8, q=128)
                o_swaT = bufA[:, 4096:8192].rearrange("p (j t) -> p j t", t=512)
                x2acc = bufA[:].bitcast(F32).rearrange("p (t c) -> p t c", c=1024)
                yT = sbuf(st3, "yT", [128, 8, 512], BF16)
                uT = sbuf(st3, "uT", [128, 8, 512], BF16)
                NS = 3
                ring = [sbuf(st3, f"ring{i}", [128, 8, 512], BF16) for i in range(NS)]
                grow_sb = sbuf(st3, "grow_sb", [128, 3, 1024], F32)
                wpp = sbuf(st3, "wpp", [128, 2, 1024], BF16)
                swa_kT = sbuf(st3, "swa_kT", [128, 8 * 128], BF16)
                Vs = sbuf(st3, "Vs", [128, 8, 2, 65], BF16)
                xn3s = [sbuf(st3, f"xn3_{i}", [128, D], BF16) for i in range(2)]
                junk3 = sbuf(st3, "junk3", [128, D], BF16)
                ropeS_g = sbuf(st3, "ropeS_g", [128, 4, 64], F32)
                qr_tms = [sbuf(st3, f"qr_tm{i}", [128, 8, 2, 64], BF16) for i in range(2)]
                kr_tms = [sbuf(st3, f"kr_tm{i}", [128, 2, 2, 32], BF16) for i in range(2)]
                PTs = [sbuf(st3, f"PTs{i}", [128, 1024], BF16) for i in range(4)]
                esink = sbuf(st3, "esink", [128, 16], F32)
                dn = sbuf(st3, "dn", [128, 16], F32)
                o_tm = sbuf(st3, "o_tm", [128, 16, 64], BF16)
                one_b = sbuf(st3, "one_b", [128, 1], F32)
                mt = [sbuf(st3, f"mt{i}", [128, 512], F32) for i in range(2)]
                stg = [sbuf(st3, f"stg{i}", [128, D], F32) for i in range(2)]
                stg2 = sbuf(st3, "stg2", [128, D], F32)
                rtmp = [stg2[:, i * 256:(i + 1) * 256].rearrange("p (h c) -> p h c", c=32) for i in range(4)]
                pt_sbs = [sbuf(st3, f"pt_sb{i}", [128, 256], F32) for i in range(2)]
                pn_sb = sbuf(st3, "pn_sb", [128, 256], BF16)
                pT_sb = sbuf(st3, "pT_sb", [128, 2, 512], BF16)
                ss3 = sbuf(st3, "ss3", [128, 8], F32)
                ln3 = sbuf(st3, "ln3", [128, 8], F32)
                rs3 = sbuf(st3, "rs3", [128, 8], F32)
                pT3 = psum(st3, "pT3", [128, 8, 128], BF16)
                pX = psum(st3, "pX", [128, 512], F32)
                pD = [psum(st3, f"pD{i}", [128, 1024], F32) for i in range(3)]

                dma("sp", "d_c", grow_sb[:], grow, writes=["grow"])
                dma("pool", "d_w3", wpp[:], w_ple.rearrange("(kc p) c -> p kc c", p=128), writes=["wpp"])
                dma("sp", "d_c", esink[:], sinks_b, writes=["esink0"])
                act(esink[:], esink[:], AF.Exp, reads=["esink0"], writes=["esink"])
                memset("pool", one_b[:], 1.0, writes=["one_b"])
                memset("pool", Vs[:], 1.0, writes=["Vs"])

                def wsrc(sc, r0, c0, ncols):
                    return (sc[r0:r0 + 1024, c0:c0 + ncols].rearrange("(kc p) c -> p kc c", p=128), ncols)

                per_group = [wsrc(sc_in, 0, 416, 512), wsrc(sc_in, 0, 928, 512), wsrc(sc_in, 0, 1440, 256)]
                for half in range(2):
                    per_group += [wsrc(sc_in, 0, 1696 + half * 512, 512), wsrc(sc_in, 0, 2720 + half * 512, 512),
                                  wsrc(sc_upA, 0, half * 512, 512), wsrc(sc_upB, 0, half * 512, 512)]
                per_group += [wsrc(sc_out, 0, 0, 512), wsrc(sc_out, 0, 512, 512)]
                for s_ in range(4):
                    per_group += [wsrc(sc_w1, 0, s_ * 1024, 512), wsrc(sc_w1, 0, s_ * 1024 + 512, 512),
                                  wsrc(sc_w2, s_ * 1024, 0, 512), wsrc(sc_w2, s_ * 1024, 512, 512)]
                per_group += [wsrc(sc_pg, 0, 0, 512), wsrc(sc_pg, 0, 512, 512)]
                srcs = per_group * NG
                rs = {"load": 0, "use": 0}

                def ring_load():
                    i = rs["load"]
                    if i >= len(srcs):
                        return
                    slot = i % NS
                    src, ncols = srcs[i]
                    dma("sp", f"d_ring{slot}", ring[slot][:, :, 0:ncols], src, reads=["scratch"], writes=[f"ring{slot}"])
                    rs["load"] += 1

                def ring_get():
                    i = rs["use"]
                    assert i < rs["load"]
                    rs["use"] += 1
                    return ring[i % NS], f"ring{i % NS}"

                def ring_release(n=1):
                    for _ in range(n):
                        ring_load()

                for _ in range(NS):
                    ring_load()

                pd_i = [0]

                def next_pd():
                    i = pd_i[0] % 3
                    pd_i[0] += 1
                    return pD[i], f"pD{i}"

                st_i = [0]

                def rms_tile(src_ap, src_key, col):
                    act(junk3[:], src_ap, AF.Square, reads=[src_key], writes=["junk3", f"ss3_{col}"], accum_out=ss3[:, col:col + 1])
                    rstd_from_ss(ss3[:, col:col + 1], rs3[:, col:col + 1], ln3[:, col:col + 1], 1.0 / D, f"ss3_{col}", f"rs3_{col}", f"ln3_{col}")

                xn_i = [0]

                def to_hT(t, gcols, src_ap, src_key, col):
                    xn3 = xn3s[xn_i[0] % 2]
                    xk = f"xn3_{xn_i[0] % 2}"
                    xn_i[0] += 1
                    if col is not None:
                        ts("dve", xn3[:], src_ap, rs3[:, col:col + 1], None, ALU.mult, reads=[src_key, f"rs3_{col}"], writes=[xk])
                    else:
                        tcopy("dve", xn3[:], src_ap, reads=[src_key], writes=[xk])
                    for kc in range(8):
                        tr(pT3[:, kc, :], xn3[:, kc * 128:(kc + 1) * 128], identb[:], reads=[xk, "identb"], writes=["pT3"])
                    if gcols is not None:
                        tt("dve", hT3[:, :, t * 128:(t + 1) * 128], pT3[:], gcol_sb[:, gcols:gcols + 8].unsqueeze(2).broadcast_to([128, 8, 128]),
                           ALU.mult, reads=["pT3", "gcol"], writes=[f"hT3_{t}"])
                    else:
                        tcopy("dve", hT3[:, :, t * 128:(t + 1) * 128], pT3[:], reads=["pT3"], writes=[f"hT3_{t}"])

                def sigmoid_from(ps_ap, ps_key, tmp_ap, tmp_key, out_ap, out_key):
                    act(tmp_ap, ps_ap, AF.Exp, reads=[ps_key], writes=[tmp_key], scale=-1.0)
                    act(tmp_ap, tmp_ap, AF.Ln, reads=[tmp_key, "one_b"], writes=[tmp_key], bias=one_b[:, 0:1], scale=1.0)
                    act(out_ap, tmp_ap, AF.Exp, reads=[tmp_key], writes=[out_key], scale=-1.0)

                def s1_tile(g_, t):
                    T = 4 * g_ + t
                    dma("sp", f"d_xg{t}", x_g[:, t, :], x[T * 128:(T + 1) * 128, :], writes=[f"x_g{t}"])
                    rms_tile(x_g[:, t, :], f"x_g{t}", t)
                    to_hT(t, 0, x_g[:, t, :], f"x_g{t}", t)

                for g in range(NG):
                    gs = slice(g * 512, (g + 1) * 512)
                    dma("sp", "d_rs", ropeS_g[:], ropeS[:, 4 * g:4 * g + 4, :], writes=["ropeS_g"])
                    if g == 0:
                        for t in range(4):
                            s1_tile(0, t)
                    wlo, wlok = ring_get()
                    whi, whik = ring_get()
                    wkv, wkvk = ring_get()
                    def s2_A(t):
                        T = 4 * g + t
                        qr_tm, qrk = qr_tms[t % 2], f"qr_tm{t % 2}"
                        kr_tm, krk = kr_tms[t % 2], f"kr_tm{t % 2}"
                        tsl = slice(t * 128, (t + 1) * 128)
                        cosb = ropeS_g[:, t, 0:32]
                        sinb = ropeS_g[:, t, 32:64]
                        for hi_, (wr, wrk) in enumerate(((wlo, wlok), (whi, whik))):
                            pd, pdk = next_pd()
                            for kc in range(8):
                                mm(pd[:, 0:512], hT3[:, kc, tsl], wr[:, kc, 0:512], start=(kc == 0), stop=(kc == 7),
                                   reads=[f"hT3_{t}", wrk], writes=[pdk])
                            q4 = pd[:, 0:512].rearrange("p (h two c) -> p h two c", two=2, c=32)
                            cb = cosb.unsqueeze(1).broadcast_to([128, 8, 32])
                            sb_ = sinb.unsqueeze(1).broadcast_to([128, 8, 32])
                            tt("dve", rtmp[0], q4[:, :, 0, :], cb, ALU.mult, reads=[pdk, "ropeS_g"], writes=["rtmp0"])
                            tt("dve", rtmp[1], q4[:, :, 1, :], sb_, ALU.mult, reads=[pdk, "ropeS_g"], writes=["rtmp1"])
                            tt("dve", rtmp[2], q4[:, :, 1, :], cb, ALU.mult, reads=[pdk, "ropeS_g"], writes=["rtmp2"])
                            tt("dve", rtmp[3], q4[:, :, 0, :], sb_, ALU.mult, reads=[pdk, "ropeS_g"], writes=["rtmp3"])
                            tt("pool", qr_tm[:, :, hi_, 0:32], rtmp[0], rtmp[1], ALU.subtract, reads=["rtmp0", "rtmp1"], writes=[qrk])
                            tt("pool", qr_tm[:, :, hi_, 32:64], rtmp[2], rtmp[3], ALU.add, reads=["rtmp2", "rtmp3"], writes=[qrk])
                        pd, pdk = next_pd()
                        for kc in range(8):
                            mm(pd[:, 0:256], hT3[:, kc, tsl], wkv[:, kc, 0:256], start=(kc == 0), stop=(kc == 7),
                               reads=[f"hT3_{t}", wkvk], writes=[pdk])
                        k4 = pd[:, 0:128].rearrange("p (h two c) -> p h two c", two=2, c=32)
                        cb2 = cosb.unsqueeze(1).broadcast_to([128, 2, 32])
                        sb2 = sinb.unsqueeze(1).broadcast_to([128, 2, 32])
                        tt("dve", rtmp[0][:, 0:2, :], k4[:, :, 0, :], cb2, ALU.mult, reads=[pdk, "ropeS_g"], writes=["rtmp0"])
                        tt("dve", rtmp[1][:, 0:2, :], k4[:, :, 1, :], sb2, ALU.mult, reads=[pdk, "ropeS_g"], writes=["rtmp1"])
                        tt("dve", rtmp[2][:, 0:2, :], k4[:, :, 1, :], cb2, ALU.mult, reads=[pdk, "ropeS_g"], writes=["rtmp2"])
                        tt("dve", rtmp[3][:, 0:2, :], k4[:, :, 0, :], sb2, ALU.mult, reads=[pdk, "ropeS_g"], writes=["rtmp3"])
                        tt("pool", kr_tm[:, :, 0, :], rtmp[0][:, 0:2, :], rtmp[1][:, 0:2, :], ALU.subtract, reads=["rtmp0", "rtmp1"], writes=[krk])
                        tt("pool", kr_tm[:, :, 1, :], rtmp[2][:, 0:2, :], rtmp[3][:, 0:2, :], ALU.add, reads=["rtmp2", "rtmp3"], writes=[krk])
                        slot = T % 8
                        tcopy("dve", Vs[:, slot, :, 0:64], pd[:, 128:256].rearrange("p (h c) -> p h c", c=64), reads=[pdk], writes=["Vs"])
                    def s2_B(t):
                        T = 4 * g + t
                        slot = T % 8
                        qr_tm, qrk = qr_tms[t % 2], f"qr_tm{t % 2}"
                        kr_tm, krk = kr_tms[t % 2], f"kr_tm{t % 2}"
                        for i in range(8):
                            tr(pT3[:, i, :], qr_tm[:, i, :, :], identb[:], reads=[qrk, "identb"], writes=["pT3"])
                        tcopy("dve", swa_qT[:, t, :, :], pT3[:], reads=["pT3"], writes=["bufA_q"])
                        tr(pT3[:, 0, :], kr_tm[:], identb[:], reads=[krk, "identb"], writes=["pT3"])
                        tcopy("dve", swa_kT[:, slot * 128:(slot + 1) * 128], pT3[:, 0, :], reads=["pT3"], writes=["swa_kT"])
                    s2_A(0)
                    for t in range(4):
                        if t + 1 < 4:
                            s2_A(t + 1)
                        s2_B(t)
                    ring_release(3)
                    def oslot(hh):
                        if hh < 7:
                            return pD[2], "pD2", hh * 65
                        if hh < 14:
                            return pD[2], "pD2", 512 + (hh - 7) * 65
                        return pX, "pX", (hh - 14) * 65

                    def ulist_of(t):
                        n = 4 * g + t
                        return ([(n - 1, maskP_sb, "maskP")] if n > 0 else []) + [(n, maskD_sb, "maskD")]

                    def s3_A(t, kvh):
                        ks = slice(kvh * 64, (kvh + 1) * 64)
                        for ui, (kb, msk, mskk) in enumerate(ulist_of(t)):
                            pd, pdk = pD[ui], f"pD{ui}"
                            sl = kb % 8
                            for half in range(2):
                                mm(pd[:, half * 512:(half + 1) * 512], swa_kT[ks, sl * 128:(sl + 1) * 128],
                                   swa_qT[ks, t, 4 * half:4 * half + 4, :], start=True, stop=False,
                                   reads=["swa_kT", "bufA_q"], writes=[pdk])
                                for i in range(4 * half, 4 * half + 4):
                                    mm(pd[:, i * 128:(i + 1) * 128], identb[:], msk[:], start=False, stop=(i % 4 == 3),
                                       reads=["identb", mskk], writes=[pdk])
                            ptile, ptk = PTs[kvh * 2 + ui], f"PTs{kvh * 2 + ui}"
                            act(ptile[:], pd[:], AF.Exp, reads=[pdk], writes=[ptk], scale=SC_SWA)

                    def s3_B(t, kvh):
                        ul = ulist_of(t)
                        for i in range(8):
                            hh = kvh * 8 + i
                            ot, otk, oc = oslot(hh)
                            for ui, (kb, msk, mskk) in enumerate(ul):
                                mm(ot[:, oc:oc + 65], PTs[kvh * 2 + ui][:, i * 128:(i + 1) * 128], Vs[:, kb % 8, kvh, :],
                                   start=(ui == 0), stop=(ui == len(ul) - 1), reads=[f"PTs{kvh * 2 + ui}", "Vs"], writes=[otk])

                    def s3_C(t):
                        tsl = slice(t * 128, (t + 1) * 128)
                        grp = [(pD[2], "pD2", 0, 0, 7), (pD[2], "pD2", 512, 7, 7), (pX, "pX", 0, 14, 2)]
                        for (ot, otk, base, h0, nh) in grp:
                            V3 = ot[:, base:base + nh * 65].rearrange("p (h c) -> p h c", c=65)
                            tt("dve", dn[:, h0:h0 + nh], V3[:, :, 64], esink[:, h0:h0 + nh], ALU.add, reads=[otk, "esink"], writes=["dn"])
                        P.op("dve", lambda hh_: hh_.reciprocal(out=dn[:], in_=dn[:]), reads=["dn"], writes=["dn"])
                        for (ot, otk, base, h0, nh) in grp:
                            V3 = ot[:, base:base + nh * 65].rearrange("p (h c) -> p h c", c=65)
                            tt("dve", o_tm[:, h0:h0 + nh, :], V3[:, :, 0:64], dn[:, h0:h0 + nh].unsqueeze(2).broadcast_to([128, nh, 64]), ALU.mult,
                               reads=[otk, "dn"], writes=["o_tm"])
                        for j in range(8):
                            tr(pT3[:, j, :], o_tm[:, 2 * j:2 * j + 2, :], identb[:], reads=["o_tm", "identb"], writes=["pT3"])
                        tcopy("dve", o_swaT[:, :, tsl], pT3[:], reads=["pT3"], writes=["bufA_o"])

                    s3_A(0, 0)
                    s3_A(0, 1)
                    for t in range(4):
                        s3_B(t, 0)
                        s3_B(t, 1)
                        if t + 1 < 4:
                            s3_A(t + 1, 0)
                            s3_A(t + 1, 1)
                        s3_C(t)
                    sA = uT[:, 0:4, :]
                    sB = uT[:, 4:8, :]
                    for half in range(2):
                        for (sdst, sk_) in ((sA, "uTa"), (sB, "uTb")):
                            wr, wrk = ring_get()
                            for jj in range(4):
                                pd, pdk = next_pd()
                                for kc in range(8):
                                    mm(pd[:, 0:512], wr[:, kc, jj * 128:(jj + 1) * 128], hT3[:, kc, :], start=(kc == 0), stop=(kc == 7),
                                       reads=[wrk, "hT3_0", "hT3_1", "hT3_2", "hT3_3"], writes=[pdk])
                                m_ = mt[jj % 2]
                                sigmoid_from(pd[:, 0:512], pdk, m_[:], f"mt{jj % 2}", sdst[:, jj, :], sk_)
                            ring_release(1)
                        wa, wak = ring_get()
                        wb, wbk = ring_get()
                        for jj in range(4):
                            pd, pdk = next_pd()
                            for kc in range(8):
                                mm(pd[:, 0:512], wa[:, kc, jj * 128:(jj + 1) * 128], o_mlaT[:, kc, gs], start=(kc == 0), stop=(kc == 7),
                                   reads=[wak, "o_mlaT"], writes=[pdk])
                            for kc in range(8):
                                mm(pd[:, 512:1024], wb[:, kc, jj * 128:(jj + 1) * 128], o_swaT[:, kc, :], start=(kc == 0), stop=(kc == 7),
                                   reads=[wbk, "bufA_o"], writes=[pdk])
                            tt("dve", mt[0][:], pd[:, 0:512], sA[:, jj, :], ALU.mult, reads=[pdk, "uTa"], writes=["mt0"])
                            tt("dve", mt[1][:], pd[:, 512:1024], sB[:, jj, :], ALU.mult, reads=[pdk, "uTb"], writes=["mt1"])
                            tt("pool", yT[:, half * 4 + jj, :], mt[0][:], mt[1][:], ALU.add, reads=["mt0", "mt1"], writes=["yT"])
                        ring_release(2)
                    w0, w0k = ring_get()
                    w1_, w1k = ring_get()
                    for t in range(4):
                        tsl = slice(t * 128, (t + 1) * 128)
                        pd, pdk = next_pd()
                        for ch, (wr, wrk) in enumerate(((w0, w0k), (w1_, w1k))):
                            for kc in range(8):
                                mm(pd[:, ch * 512:(ch + 1) * 512], yT[:, kc, tsl], wr[:, kc, 0:512], start=(kc == 0), stop=(kc == 7),
                                   reads=["yT", wrk], writes=[pdk])
                        rms_tile(pd[:], pdk, 4 + t)
                        stt("dve", stg2[:], pd[:], rs3[:, 4 + t:5 + t], grow_sb[:, 0, :], ALU.mult, ALU.mult,
                            reads=[pdk, f"rs3_{4 + t}", "grow", "rtmp0", "rtmp1", "rtmp2", "rtmp3"], writes=["rtmp0", "rtmp1", "rtmp2", "rtmp3"])
                        tt("dve", x_g[:, t, :], x_g[:, t, :], stg2[:], ALU.add, reads=[f"x_g{t}", "rtmp0", "rtmp1", "rtmp2", "rtmp3"], writes=[f"x_g{t}"])
                    ring_release(2)
                    for t in range(4):
                        rms_tile(x_g[:, t, :], f"x_g{t}", t)
                        to_hT(t, 8, x_g[:, t, :], f"x_g{t}", t)
                    for s_ in range(4):
                        for cidx in range(2):
                            wr, wrk = ring_get()
                            for jj in range(4):
                                fc = cidx * 4 + jj
                                pd, pdk = next_pd()
                                for kc in range(8):
                                    mm(pd[:, 0:512], wr[:, kc, jj * 128:(jj + 1) * 128], hT3[:, kc, :], start=(kc == 0), stop=(kc == 7),
                                       reads=[wrk, "hT3_0", "hT3_1", "hT3_2", "hT3_3"], writes=[pdk])
                                m_, mk_ = mt[jj % 2], f"mt{jj % 2}"
                                act(m_[:], pd[:, 0:512], AF.Relu, reads=[pdk], writes=[mk_])
                                uk = "uTa" if fc < 4 else "uTb"
                                tt("pool", uT[:, fc, :], m_[:], m_[:], ALU.mult, reads=[mk_], writes=[uk])
                            ring_release(1)
                        for ch in range(2):
                            wr, wrk = ring_get()
                            for t in range(4):
                                tsl = slice(t * 128, (t + 1) * 128)
                                pd, pdk = next_pd()
                                for fc in range(8):
                                    mm(pd[:, 0:512], uT[:, fc, tsl], wr[:, fc, 0:512], start=(fc == 0), stop=(fc == 7),
                                       reads=["uTa", "uTb", wrk], writes=[pdk])
                                ak = "bufA_q" if t < 2 else "bufA_o"
                                dst = x2acc[:, t, ch * 512:(ch + 1) * 512]
                                if s_ == 0:
                                    tcopy("dve", dst, pd[:, 0:512], reads=[pdk], writes=[ak])
                                else:
                                    tt("dve", dst, pd[:, 0:512], dst, ALU.add, reads=[pdk, ak], writes=[ak])
                            ring_release(1)
                    for t in range(4):
                        ak = "bufA_q" if t < 2 else "bufA_o"
                        rms_tile(x2acc[:, t, :], ak, 4 + t)
                        stt("dve", stg2[:], x2acc[:, t, :], rs3[:, 4 + t:5 + t], grow_sb[:, 1, :], ALU.mult, ALU.mult,
                            reads=[ak, f"rs3_{4 + t}", "grow", "rtmp0", "rtmp1", "rtmp2", "rtmp3"], writes=["rtmp0", "rtmp1", "rtmp2", "rtmp3"])
                        tt("dve", x_g[:, t, :], x_g[:, t, :], stg2[:], ALU.add, reads=[f"x_g{t}", "rtmp0", "rtmp1", "rtmp2", "rtmp3"], writes=[f"x_g{t}"])
                    def s8_A(t):
                        T = 4 * g + t
                        to_hT(t, None, x_g[:, t, :], f"x_g{t}", None)
                        ptb, ptk_ = pt_sbs[t % 2], f"pt_sb{t % 2}"
                        dma("sp", f"d_p{t % 2}", ptb[:], p_in[T * 128:(T + 1) * 128, :], writes=[ptk_])
                        tcopy("dve", pn_sb[:], ptb[:], reads=[ptk_], writes=["pn_sb"])
                        for kc in range(2):
                            tr(pT3[:, kc, :], pn_sb[:, kc * 128:(kc + 1) * 128], identb[:], reads=["pn_sb", "identb"], writes=["pT3"])
                        tcopy("dve", pT_sb[:, :, t * 128:(t + 1) * 128], pT3[:, 0:2, :], reads=["pT3"], writes=[f"pT_sb{t}"])

                    w0, w0k = ring_get()
                    w1_, w1k = ring_get()

                    def s8_B(t):
                        T = 4 * g + t
                        tsl = slice(t * 128, (t + 1) * 128)
                        pa, pak = next_pd()
                        pb_, pbk = next_pd()
                        for ch, (wr, wrk) in enumerate(((w0, w0k), (w1_, w1k))):
                            for kc in range(8):
                                mm(pa[:, ch * 512:(ch + 1) * 512], hT3[:, kc, tsl], wr[:, kc, 0:512], start=(kc == 0), stop=(kc == 7),
                                   reads=[f"hT3_{t}", wrk], writes=[pak])
                            for kc in range(2):
                                mm(pb_[:, ch * 512:(ch + 1) * 512], pT_sb[:, kc, tsl], wpp[:, kc, ch * 512:(ch + 1) * 512], start=(kc == 0), stop=(kc == 1),
                                   reads=[f"pT_sb{t}", "wpp"], writes=[pbk])
                        so = stg[st_i[0] % 2]
                        sok = f"stg{st_i[0] % 2}"
                        st_i[0] += 1
                        sigmoid_from(pa[:], pak, so[:], sok, so[:], sok)
                        rms_tile(pb_[:], pbk, 4 + t)
                        stt("dve", stg2[:], pb_[:], rs3[:, 4 + t:5 + t], grow_sb[:, 2, :], ALU.mult, ALU.mult,
                            reads=[pbk, f"rs3_{4 + t}", "grow", "rtmp0", "rtmp1", "rtmp2", "rtmp3"], writes=["rtmp0", "rtmp1", "rtmp2", "rtmp3"])
                        tt("dve", so[:], so[:], stg2[:], ALU.mult, reads=[sok, "rtmp0", "rtmp1", "rtmp2", "rtmp3"], writes=[sok])
                        tt("dve", so[:], so[:], x_g[:, t, :], ALU.add, reads=[sok, f"x_g{t}"], writes=[sok])
                        dma("pool", f"d_out{st_i[0] % 2}", out[T * 128:(T + 1) * 128, :], so[:], reads=[sok], writes=["out"])

                    s8_A(0)
                    for t in range(4):
                        if t + 1 < 4:
                            s8_A(t + 1)
                        s8_B(t)
                        if g + 1 < NG and t >= 1:
                            s1_tile(g + 1, t - 1)
                    if g + 1 < NG:
                        s1_tile(g + 1, 3)
                    ring_release(2)
                P.barrier()
        P.barrier(final=True)
        P.emit()
    return nc


def _rope_tables():
    pos = np.arange(S, dtype=np.float32)
    inv16 = np.exp(np.float32(-math.log(10000.0)) * np.arange(16, dtype=np.float32) * np.float32(2.0 / 32)).astype(np.float32)
    ang16 = (pos[:, None] * inv16[None, :]).astype(np.float32)
    c16, s16 = np.cos(ang16).astype(np.float32), np.sin(ang16).astype(np.float32)
    ck = np.concatenate([c16, c16, s16, s16], axis=1)
    ropeK = ck.reshape(NT, 128, 64).transpose(1, 0, 2).copy()
    ropeQ = np.zeros((2, 128, S), np.float32)
    ropeQ[0, 64:96, :] = np.concatenate([c16, c16], axis=1).T
    ropeQ[1, 96:128, :] = np.concatenate([s16, s16], axis=1).T
    inv32 = np.exp(np.float32(-math.log(10000.0)) * np.arange(32, dtype=np.float32) * np.float32(2.0 / 64)).astype(np.float32)
    ang32 = (pos[:, None] * inv32[None, :]).astype(np.float32)
    cs = np.concatenate([np.cos(ang32), np.sin(ang32)], axis=1).astype(np.float32)
    ropeS = cs.reshape(NT, 128, 64).transpose(1, 0, 2).copy()
    return ropeK, ropeS, ropeQ


def make_in_maps(inputs):
    f = lambda a: np.ascontiguousarray(np.asarray(a, dtype=np.float32))
    x = f(inputs["x"])
    p = f(inputs["p"])[0]
    gmp, gqa, gkv = f(inputs["g_mix_pre"])[0], f(inputs["g_q_a"])[0], f(inputs["g_kv_a"])[0]
    gmlp = f(inputs["g_mlp_pre"])[0]
    gcol = np.concatenate([gmp.reshape(8, 128).T, gmlp.reshape(8, 128).T, gqa.reshape(2, 128).T, gkv.reshape(1, 128).T], axis=1)
    grow = np.stack([np.broadcast_to(f(inputs[k])[0][None, :], (128, 1024)) for k in ("g_mix_post", "g_mlp_post", "g_ple")], axis=1)
    sinks_b = np.broadcast_to(f(inputs["sinks"])[0][None, :], (128, 16))
    ropeK, ropeS, ropeQ = _rope_tables()
    kk = np.arange(128)
    NEGB = np.float32(-30000.0)
    maskD = np.where(kk[:, None] <= kk[None, :], np.float32(0), NEGB).astype(ml_dtypes.bfloat16)
    maskP = np.where(kk[:, None] > kk[None, :], np.float32(0), NEGB).astype(ml_dtypes.bfloat16)
    common = {
        "w_in": f(inputs["w_in"])[0], "w_q_b": f(inputs["w_q_b"])[0], "w_kv_b": f(inputs["w_kv_b"])[0],
        "w_mla_up": f(inputs["w_mla_up"])[0], "w_swa_up": f(inputs["w_swa_up"])[0], "w_out": f(inputs["w_out"])[0],
        "w_mlp_up": f(inputs["w_mlp_up"])[0], "w_mlp_down": f(inputs["w_mlp_down"])[0],
        "w_ple": f(inputs["w_ple"])[0], "w_ple_gate": f(inputs["w_ple_gate"])[0],
        "gcol": np.ascontiguousarray(gcol), "grow": np.ascontiguousarray(grow), "sinks_b": np.ascontiguousarray(sinks_b),
        "ident": np.eye(128, dtype=np.float32).astype(ml_dtypes.bfloat16), "maskD": maskD, "maskP": maskP,
        "ropeK": ropeK, "ropeS": ropeS, "ropeQ": ropeQ,
    }
    return [dict(common, x=np.ascontiguousarray(x[b]), p=np.ascontiguousarray(p[b])) for b in range(x.shape[0])]


def kernel(**inputs):
    in_maps = make_in_maps(inputs)
    nc = build()
    res = run_bass_kernel_spmd(nc, in_maps, core_ids=list(range(8)))
    return np.stack([r["out"] for r in res.results], axis=0).astype(np.float32)
```
